# Optimizing a Trainium2 kernel written in Bass

```python
import jax, jax.numpy as jnp
from jax import lax
import numpy as np

D_MODEL = 1024
BATCH = 4
SEQ = 8192
DEPTH = 1

CHUNK = 64

A_HEADS = 8
A_HEAD_DIM = 64
A_WIDTH = A_HEADS * A_HEAD_DIM
A_LEFT_CHUNKS = 8
A_BAND = (A_LEFT_CHUNKS + 1) * CHUNK
REL_CLIP = 128

B_HEADS = 8
B_NOPE_DIM = 64
B_ROPE_DIM = 32
B_QK_DIM = B_NOPE_DIM + B_ROPE_DIM
B_V_DIM = 64
B_WIDTH = B_HEADS * B_V_DIM
Q_LORA = 256
KV_LORA = 128
ROPE_THETA = 10000.0
Q_BLOCK = 128

DEEPNORM_ALPHA = (2 * DEPTH) ** 0.25
DEEPNORM_BETA = (8 * DEPTH) ** -0.25
LN_EPS = 1e-5
RMS_EPS = 1e-6
NEG_INF = -1e30

IN_SPLITS = (A_WIDTH, A_WIDTH, A_WIDTH, A_WIDTH,
             Q_LORA, KV_LORA, B_ROPE_DIM, B_WIDTH,
             D_MODEL, D_MODEL)
IN_COLS = sum(IN_SPLITS)

kernel_name = "hybrid_chunk_relpos_mla_deepnorm"


def _split_points():
    pts, acc = [], 0
    for w in IN_SPLITS[:-1]:
        acc += w
        pts.append(acc)
    return pts


def layer_norm(x, g, b):
    xf = x.astype(jnp.float32)
    mu = jnp.mean(xf, axis=-1, keepdims=True)
    var = jnp.mean(jnp.square(xf - mu), axis=-1, keepdims=True)
    return ((xf - mu) * lax.rsqrt(var + LN_EPS) * g.astype(jnp.float32) + b.astype(jnp.float32)).astype(x.dtype)


def rms_norm(x, g):
    xf = x.astype(jnp.float32)
    return (xf * lax.rsqrt(jnp.mean(jnp.square(xf), axis=-1, keepdims=True) + RMS_EPS) * g.astype(jnp.float32)).astype(x.dtype)


def rope(x, positions):
    half = x.shape[-1] // 2
    inv_freq = ROPE_THETA ** (-jnp.arange(half, dtype=jnp.float32) / half)
    ang = positions.astype(jnp.float32)[..., None] * inv_freq
    cos = jnp.cos(ang)[:, :, None, :]
    sin = jnp.sin(ang)[:, :, None, :]
    x1 = x[..., :half].astype(jnp.float32)
    x2 = x[..., half:].astype(jnp.float32)
    return jnp.concatenate([x1 * cos - x2 * sin, x2 * cos + x1 * sin], axis=-1).astype(x.dtype)


def chunked_relpos_attention(q, k, v, rel_bias):
    B, S, H, Dh = q.shape
    n_chunks = S // CHUNK
    pad = A_LEFT_CHUNKS * CHUNK
    k_pad = jnp.pad(k, ((0, 0), (pad, 0), (0, 0), (0, 0)))
    v_pad = jnp.pad(v, ((0, 0), (pad, 0), (0, 0), (0, 0)))
    rel = jnp.clip(jnp.arange(CHUNK)[:, None] + pad - jnp.arange(A_BAND)[None, :], -REL_CLIP, REL_CLIP) + REL_CLIP
    bias = jnp.transpose(rel_bias[rel], (2, 0, 1)).astype(jnp.float32)
    scale = Dh ** -0.5

    def one_chunk(c):
        start = c * CHUNK
        qc = lax.dynamic_slice_in_dim(q, start, CHUNK, axis=1)
        kc = lax.dynamic_slice_in_dim(k_pad, start, A_BAND, axis=1)
        vc = lax.dynamic_slice_in_dim(v_pad, start, A_BAND, axis=1)
        s = jnp.einsum('bqhd,bkhd->bhqk', qc, kc).astype(jnp.float32) * scale + bias
        valid = (start - pad + jnp.arange(A_BAND)) >= 0
        s = jnp.where(valid[None, None, None, :], s, NEG_INF)
        p = jax.nn.softmax(s, axis=-1).astype(vc.dtype)
        return jnp.einsum('bhqk,bkhd->bqhd', p, vc)

    out = lax.map(one_chunk, jnp.arange(n_chunks))
    return jnp.transpose(out, (1, 0, 2, 3, 4)).reshape(B, S, H, Dh)


def chunk_causal_attention(q, k, v):
    B, S, H, Dqk = q.shape
    n_blocks = S // Q_BLOCK
    scale = Dqk ** -0.5
    key_chunk = jnp.arange(S) // CHUNK

    def one_block(i):
        start = i * Q_BLOCK
        qb = lax.dynamic_slice_in_dim(q, start, Q_BLOCK, axis=1)
        s = jnp.einsum('bqhd,bkhd->bhqk', qb, k).astype(jnp.float32) * scale
        q_chunk = (start + jnp.arange(Q_BLOCK)) // CHUNK
        mask = key_chunk[None, :] <= q_chunk[:, None]
        s = jnp.where(mask[None, None], s, NEG_INF)
        p = jax.nn.softmax(s, axis=-1).astype(v.dtype)
        return jnp.einsum('bhqk,bkhd->bqhd', p, v)

    out = lax.map(one_block, jnp.arange(n_blocks))
    return jnp.transpose(out, (1, 0, 2, 3, 4)).reshape(B, S, H, v.shape[-1])


def setup_inputs(seed: int = 0) -> dict:
    key = jax.random.key(seed)
    ks = jax.random.split(key, 16)
    f32 = jnp.float32
    nrm = lambda k, shape, s: jax.random.normal(k, shape, f32) * s
    return {
        "x": jax.random.normal(ks[0], (BATCH, SEQ, D_MODEL), f32),
        "positions": jnp.broadcast_to(jnp.arange(SEQ, dtype=jnp.int32), (BATCH, SEQ)),
        "ln_in_g": 1.0 + nrm(ks[1], (D_MODEL,), 0.01),
        "ln_in_b": nrm(ks[2], (D_MODEL,), 0.01),
        "w_in": nrm(ks[3], (DEPTH, D_MODEL, IN_COLS), D_MODEL ** -0.5),
        "b_in": nrm(ks[4], (DEPTH, IN_COLS), 0.01),
        "q_norm_g": 1.0 + nrm(ks[5], (DEPTH, Q_LORA), 0.01),
        "kv_norm_g": 1.0 + nrm(ks[6], (DEPTH, KV_LORA), 0.01),
        "w_uq": nrm(ks[7], (DEPTH, Q_LORA, B_HEADS * B_QK_DIM), Q_LORA ** -0.5),
        "w_ukv": nrm(ks[8], (DEPTH, KV_LORA, B_HEADS * (B_NOPE_DIM + B_V_DIM)), KV_LORA ** -0.5),
        "rel_bias": nrm(ks[9], (DEPTH, 2 * REL_CLIP + 1, A_HEADS), 0.2),
        "w_proj_a": nrm(ks[10], (DEPTH, A_WIDTH, D_MODEL), A_WIDTH ** -0.5 * DEEPNORM_BETA),
        "w_proj_b": nrm(ks[11], (DEPTH, B_WIDTH, D_MODEL), B_WIDTH ** -0.5 * DEEPNORM_BETA),
        "w_out": nrm(ks[12], (DEPTH, D_MODEL, D_MODEL), D_MODEL ** -0.5 * DEEPNORM_BETA),
        "ln_post_g": 1.0 + nrm(ks[13], (DEPTH, D_MODEL), 0.01),
        "ln_post_b": nrm(ks[14], (DEPTH, D_MODEL), 0.01),
    }


def reference(x, positions, ln_in_g, ln_in_b, w_in, b_in, q_norm_g, kv_norm_g, w_uq, w_ukv,
              rel_bias, w_proj_a, w_proj_b, w_out, ln_post_g, ln_post_b):
    B, S, _ = x.shape
    h = layer_norm(x, ln_in_g, ln_in_b)
    pts = _split_points()
    for l in range(DEPTH):
        proj = h @ w_in[l] + b_in[l]
        a_q, a_k, a_v, a_z, b_cq, b_ckv, b_kr, b_z, g_a, g_b = jnp.split(proj, pts, axis=-1)

        ya = chunked_relpos_attention(a_q.reshape(B, S, A_HEADS, A_HEAD_DIM),
                                      a_k.reshape(B, S, A_HEADS, A_HEAD_DIM),
                                      a_v.reshape(B, S, A_HEADS, A_HEAD_DIM), rel_bias[l])
        ya = (ya.reshape(B, S, A_WIDTH) * jax.nn.silu(a_z)) @ w_proj_a[l]

        cq = rms_norm(b_cq, q_norm_g[l])
        qb = (cq @ w_uq[l]).reshape(B, S, B_HEADS, B_QK_DIM)
        q_full = jnp.concatenate([qb[..., :B_NOPE_DIM], rope(qb[..., B_NOPE_DIM:], positions)], axis=-1)
        ckv = rms_norm(b_ckv, kv_norm_g[l])
        kv = (ckv @ w_ukv[l]).reshape(B, S, B_HEADS, B_NOPE_DIM + B_V_DIM)
        k_pe = rope(b_kr[:, :, None, :], positions)
        k_full = jnp.concatenate([kv[..., :B_NOPE_DIM],
                                  jnp.broadcast_to(k_pe, (B, S, B_HEADS, B_ROPE_DIM))], axis=-1)
        yb = chunk_causal_attention(q_full, k_full, kv[..., B_NOPE_DIM:])
        yb = (yb.reshape(B, S, B_WIDTH) * jax.nn.silu(b_z)) @ w_proj_b[l]

        mixed = jax.nn.sigmoid(g_a) * ya + jax.nn.sigmoid(g_b) * yb
        out = mixed @ w_out[l]
        h = layer_norm(DEEPNORM_ALPHA * h + out, ln_post_g[l], ln_post_b[l])
    return h
```

```python
import math
from contextlib import ExitStack

import numpy as np
import concourse.bass as bass
import concourse.mybir as mybir
from concourse.bass_utils import run_bass_kernel_spmd

F32 = mybir.dt.float32
BF16 = mybir.dt.bfloat16
I32 = mybir.dt.int32
AF = mybir.ActivationFunctionType
ALU = mybir.AluOpType

D = 1024
SEQ = 8192
BATCH = 4
IN_COLS = 5024
C_AQ, C_AK, C_AV, C_AZ, C_CQ, C_CKV, C_KR, C_BZ, C_GA, C_GB = 0, 512, 1024, 1536, 2048, 2304, 2432, 2464, 2976, 4000
ALPHA = 2.0 ** 0.25
LN_EPS = 1e-5
RMS_EPS = 1e-6
NEG = -30000.0
TWO_PI = 2.0 * math.pi

ENGS = ("tensor", "vector", "scalar", "gpsimd", "sync")
SEM_EPOCH = 30000


class Op:
    __slots__ = ("eng", "name", "args", "kw", "deps", "signal", "idx", "dma", "dsem", "dval")

    def __init__(self, eng, name, args, kw, dma):
        self.eng = eng
        self.name = name
        self.args = args
        self.kw = kw
        self.deps = []
        self.signal = False
        self.idx = -1
        self.dma = dma
        self.dsem = None
        self.dval = 0


class Sched:
    def __init__(self):
        self.q = {e: [] for e in ENGS}
        self.last_w = {}
        self.readers = {}
        self.dma_slots = {}
        self.slot_names = []
        self.all_dma = []
        self.pending = {e: [] for e in ENGS}
        self.ps_read = {}

    def op(self, eng, name, *args, reads=(), writes=(), dma_slot=None, **kw):
        o = Op(eng, name, args, kw, dma_slot is not None)
        o.idx = len(self.q[eng])
        deps = list(self.pending[eng])
        self.pending[eng] = []
        ex = [r for r in reads if isinstance(r, tuple) and r[0] == "ps"]
        if ex:
            reads = [r for r in reads if not (isinstance(r, tuple) and r[0] == "ps")]
            for r in ex:
                w = self.last_w.get(r)
                if w is not None and not (w.eng == eng and self.ps_read.get(r, False) and not w.dma):
                    deps.append(w)
                deps.extend(self.readers.get(r, ()))
                self.last_w[r] = o
                self.readers[r] = []
                self.ps_read[r] = True
        for r in writes:
            if isinstance(r, tuple) and r[0] == "ps":
                self.ps_read[r] = False
        for r in reads:
            w = self.last_w.get(r)
            if w is not None:
                deps.append(w)
        for r in writes:
            w = self.last_w.get(r)
            if w is not None:
                deps.append(w)
            deps.extend(self.readers.get(r, ()))
        for r in reads:
            lst = self.readers.setdefault(r, [])
            if not o.dma:
                lst[:] = [x for x in lst if x.dma or x.eng != eng]
            lst.append(o)
        for r in writes:
            self.last_w[r] = o
            self.readers[r] = []
        seen = set()
        for d in deps:
            if d is o or id(d) in seen:
                continue
            seen.add(id(d))
            if (not d.dma) and d.eng == eng and eng == "tensor":
                continue
            o.deps.append(d)
            d.signal = True
        if dma_slot is not None:
            if dma_slot not in self.dma_slots:
                self.dma_slots[dma_slot] = 0
                self.slot_names.append(dma_slot)
            self.dma_slots[dma_slot] += 16
            o.dsem = dma_slot
            o.dval = self.dma_slots[dma_slot]
            self.all_dma.append(o)
        self.q[eng].append(o)
        return o

    def barrier(self):
        lasts = []
        for e in ENGS:
            comp = [o for o in self.q[e] if not o.dma]
            if comp:
                lasts.append(comp[-1])
        last_dma = {}
        for o in self.all_dma:
            last_dma[o.dsem] = o
        lasts.extend(last_dma.values())
        for e in ENGS:
            self.pending[e] = list(self.pending[e]) + lasts

    def emit(self, nc, es, final_waits=()):
        n_sig = {e: sum(1 for o in self.q[e] if o.signal and not o.dma) for e in ENGS}
        esems = {}
        for e in ENGS:
            n_ep = n_sig[e] // SEM_EPOCH + 1
            esems[e] = [es.enter_context(nc.semaphore(f"s_{e}_{k}")) for k in range(n_ep)]
        dsems = {}
        for i, s in enumerate(self.slot_names):
            dsems[s] = es.enter_context(nc.semaphore(f"d_{i}"))
        for e in ENGS:
            c = 0
            for o in self.q[e]:
                if (not o.dma) and o.signal:
                    c += 1
                    o.dval = c

        def ev_of(d):
            if d.dma:
                return dsems[d.dsem], d.dval, None, 0
            ep = (d.dval - 1) // SEM_EPOCH
            return esems[d.eng][ep], d.dval - ep * SEM_EPOCH, d.eng, ep

        block = es.enter_context(nc.Block())
        sched = self

        def make(e):
            def body(eng):
                waited = {}
                max_ep = {}
                for o in sched.q[e]:
                    for d in o.deps:
                        sem, val, deng, ep = ev_of(d)
                        if deng is not None and max_ep.get(deng, -1) > ep:
                            continue
                        if waited.get(sem.name, 0) >= val:
                            continue
                        waited[sem.name] = val
                        if deng is not None:
                            max_ep[deng] = max(max_ep.get(deng, -1), ep)
                        eng.wait_ge(sem, val)
                    ins = getattr(eng, o.name)(*o.args, **o.kw)
                    if o.dma:
                        ins.then_inc(dsems[o.dsem], 16)
                    elif o.signal:
                        ep = (o.dval - 1) // SEM_EPOCH
                        ins.then_inc(esems[e][ep], 1)
                if e == "sync":
                    for d in final_waits:
                        sem, val, _, _ = ev_of(d)
                        eng.wait_ge(sem, val)
            return body

        block.tensor(make("tensor"))
        block.vector(make("vector"))
        block.scalar(make("scalar"))
        block.gpsimd(make("gpsimd"))
        block.sync(make("sync"))


def build_nc(S=SEQ, dbg=False, upto=3, p1cut=9, p3cut=9):
    assert S % 1024 == 0
    NG = S // 512
    NP = NG // 2
    SO = S // 2
    NT = S // 128

    nc = bass.Bass("TRN2", target_bir_lowering=False, dynamic_dma_scratch_size=1024)

    def dram(n, sh, dt, kind="ExternalInput"):
        return nc.dram_tensor(n, sh, dt, kind=kind).ap()

    x_d = dram("x", [S, D], F32)
    pos_d = dram("pos32", [32, S], I32)
    ident_d = dram("ident", [128, 128], F32)
    bcol_d = dram("bcol", [128, 44], F32)
    ccol_d = dram("ccol", [128, 8], F32)
    lncol_d = dram("lncol", [128, 16], F32)
    lnbc_d = dram("lnbc", [128, 4, D], F32)
    bt_d = dram("bt", [128, 8, 640], F32)
    amask_d = dram("amask", [128, 640], F32)
    w_in_d = dram("w_in", [D, IN_COLS], F32)
    w_krp_d = dram("w_krp", [D, 192], F32)
    w_uq_d = dram("w_uq", [256, 768], F32)
    w_uqs_d = dram("w_uq_sw", [256, 768], F32)
    w_ukv_d = dram("w_ukv", [128, 1024], F32)
    w_pa_d = dram("w_proj_a", [512, D], F32)
    w_pb_d = dram("w_proj_b", [512, D], F32)
    w_out_d = dram("w_out", [D, D], F32)
    out_d = dram("out", [SO, D], F32, kind="ExternalOutput")
    CH = []
    for c in range(4):
        CH += [("in", C_AK + c * 128, 8), ("in", C_AQ + c * 128, 8), ("in", C_AV + c * 128, 8), ("in", C_AZ + c * 128, 8)]
    for c in range(4):
        CH.append(("in", C_BZ + c * 128, 8))
    for oc in range(8):
        CH += [("in", C_GA + oc * 128, 8), ("pa", oc * 128, 4), ("in", C_GB + oc * 128, 8), ("pb", oc * 128, 4)]
    for cc in range(8):
        CH.append(("out", cc * 128, 8))
    NCH = len(CH)
    wscr_d = dram("wscr", [NCH, 128, 1024], BF16, kind="Internal")
    WSRC = {"in": w_in_d, "pa": w_pa_d, "pb": w_pb_d, "out": w_out_d}
    if dbg:
        dbg_yb_d = dram("dbg_yb", [128, 4, SO], BF16, kind="ExternalOutput")

    SC = Sched()
    op = SC.op
    es = ExitStack()
    with es:
        def sbt(n, sh, dt):
            return es.enter_context(nc.sbuf_tensor(n, sh, dt))

        ident = sbt("ident_s", [128, 128], F32)
        ones_b = sbt("ones_b", [128, 128], BF16)
        bcol = sbt("bcol_s", [128, 44], F32)
        bcolh = sbt("bcolh_s", [128, 44], F32)
        ccol = sbt("ccol_s", [128, 8], F32)
        lncol = sbt("lncol_s", [128, 16], F32)
        mhalf = sbt("mhalf_s", [128, 8], F32)
        epsc = sbt("epsc_s", [128, 2], F32)
        stat = sbt("stat_s", [128, 64], F32)
        stat4 = sbt("stat4_s", [128, 64], F32)
        wst = sbt("wst_s", [128, 2, 1024], F32)
        ybT = sbt("ybT_s", [128, 4, SO], BF16)
        psum = es.enter_context(nc.psum_tensor("ps", [128, 8, 512], F32))

        P3 = 180 * 1024
        P12 = 12 * S + 72 * 1024
        ARENA = max(P12, P3)
        arena = sbt("arena", [128, ARENA // 2], BF16)

        def view(off, shape, dt):
            n = 1
            for s_ in shape[1:]:
                n *= s_
            esz = 4 if dt in (F32, I32) else 2
            assert off % 4 == 0 and off + n * esz <= ARENA, (off, shape, ARENA)
            v = arena[:, off // 2: off // 2 + n * esz // 2]
            if esz == 4:
                v = v.bitcast(dt)
            if len(shape) == 3:
                v = v.rearrange("p (a b) -> p a b", b=shape[2])
            elif len(shape) == 4:
                v = v.rearrange("p (a b c) -> p a b c", b=shape[2], c=shape[3])
            return v

        class Bump:
            def __init__(self, off=0):
                self.off = off

            def __call__(self, shape, dt):
                n = 1
                for s_ in shape[1:]:
                    n *= s_
                esz = 4 if dt in (F32, I32) else 2
                v = view(self.off, shape, dt)
                self.off += (n * esz + 31) // 32 * 32
                return v

        al = Bump(0)
        ckvT = al([128, S], BF16)
        cqT = al([128, 2, SO], BF16)
        KT = [al([128, S], BF16), al([128, S], BF16)]
        cosT = al([128, SO], F32)
        sinT = al([128, SO], F32)
        p2_base = al.off
        Vb = [al([128, NT, 128], BF16), al([128, NT, 128], BF16)]
        QT = [al([128, 512], BF16), al([128, 512], BF16)]
        PT = [al([128, 1024], BF16) for _ in range(3)]
        rcb = [al([128, 512], F32) for _ in range(2)]
        qtmp = [al([128, 512], F32) for _ in range(2)]
        wuq = al([128, 2, 768], BF16)
        wuqs = al([128, 2, 768], BF16)
        wukv = al([128, 1024], BF16)
        cvb = [al([128, 1024], BF16) for _ in range(2)]
        a1 = Bump(p2_base)
        xb1 = [a1([128, D], F32) for _ in range(4)]
        hT1 = [a1([128, 8, 512], BF16) for _ in range(2)]
        wB = a1([128, 8, 384], BF16)
        wKR = a1([128, 8, 192], BF16)
        f1 = [a1([128, 512], F32) for _ in range(2)]
        sq1 = [a1([128, 512], F32) for _ in range(2)]
        rs1 = a1([128, 512], F32)
        sqh = [a1([128, 512], BF16) for _ in range(2)]
        sql = [a1([128, 512], BF16) for _ in range(2)]
        tb1 = [a1([128, 1024], F32) for _ in range(2)]
        tbi = a1([128, 1024], I32)
        posi = [a1([128, 512], I32) for _ in range(2)]
        kr1 = [a1([128, 512], F32) for _ in range(2)]

        dcount = [0]

        def PS(b):
            return psum[:, b, :]

        def load_cast(dst, src_ap, n_elem, key, eng_cast="gpsimd"):
            slot = dcount[0] % 2
            dcount[0] += 1
            st = wst[:, slot, 0:n_elem]
            shp = dst.shape
            if len(shp) == 3:
                st = st.rearrange("p (a b) -> p a b", b=shp[2])
            op("sync", "dma_start", out=st, in_=src_ap, writes=[("wst", slot)], dma_slot=f"wst{slot}")
            if eng_cast == "scalar":
                op("scalar", "activation", dst, st, AF.Copy, reads=[("wst", slot)], writes=[key])
            else:
                op(eng_cast, "tensor_copy", dst, st, reads=[("wst", slot)], writes=[key])

        op("sync", "dma_start", out=ident[:], in_=ident_d, writes=["ident"], dma_slot="c0")
        op("sync", "dma_start", out=bcol[:], in_=bcol_d, writes=["bcol"], dma_slot="c1")
        op("sync", "dma_start", out=ccol[:], in_=ccol_d, writes=["ccol"], dma_slot="c2")
        op("sync", "dma_start", out=lncol[:], in_=lncol_d, writes=["lncol"], dma_slot="c3")
        op("gpsimd", "memset", ones_b[:], 1.0, writes=["ones"])
        op("gpsimd", "memset", mhalf[:], -0.5, writes=["mhalf"])
        op("gpsimd", "memset", epsc[:], RMS_EPS, writes=["epsc"])
        op("gpsimd", "tensor_scalar", bcolh[:], bcol[:], 0.5, None, ALU.mult, reads=["bcol"], writes=["bcolh"])
        for k in range(8):
            load_cast(wB[:, k, :], w_in_d[k * 128:(k + 1) * 128, C_CQ:C_CQ + 384], 384, ("wB", k))
            load_cast(wKR[:, k, :], w_krp_d[k * 128:(k + 1) * 128, :], 192, ("wKR", k))

        ZC = ccol[:, 7:8]
        FLAG = ccol[:, 5:6]

        def ln_stats(xt, sidx, keyx):
            keys = ("st", sidx)
            b = sidx * 16
            st6 = stat[:, b:b + 12].rearrange("p (a b) -> p a b", b=6)
            op("vector", "bn_stats", st6[:, 0, :], xt[:, 0:512], reads=[keyx], writes=[keys])
            op("vector", "bn_stats", st6[:, 1, :], xt[:, 512:1024], reads=[keyx], writes=[keys])
            op("vector", "bn_aggr", stat[:, b + 12:b + 14], stat[:, b:b + 12], reads=[keys], writes=[keys])
            op("vector", "tensor_scalar", stat[:, b + 13:b + 14], stat[:, b + 13:b + 14], LN_EPS, None, ALU.add,
               reads=[keys], writes=[keys])
            op("gpsimd", "tensor_tensor", stat[:, b + 13:b + 14], stat[:, b + 13:b + 14], mhalf[:, 0:1], ALU.pow,
               reads=[keys, "mhalf"], writes=[keys])
            op("vector", "scalar_tensor_tensor", stat[:, b + 14:b + 15], stat[:, b + 12:b + 13], -1.0,
               stat[:, b + 13:b + 14], ALU.mult, ALU.mult, reads=[keys], writes=[keys])
            return stat[:, b + 13:b + 14], stat[:, b + 14:b + 15], keys

        def ln_stats4(x4, keysx):
            k4 = "st4"
            for b in range(4):
                st6 = stat4[:, b * 12:(b + 1) * 12].rearrange("p (a b) -> p a b", b=6)
                op("vector", "bn_stats", st6[:, 0, :], x4[:, b, 0:512], reads=[keysx[b]], writes=[k4])
                op("vector", "bn_stats", st6[:, 1, :], x4[:, b, 512:1024], reads=[keysx[b]], writes=[k4])
                op("vector", "bn_aggr", stat4[:, 48 + 2 * b:50 + 2 * b], stat4[:, b * 12:(b + 1) * 12], reads=[k4], writes=[k4])
            mv = stat4[:, 48:56].rearrange("p (b t) -> p b t", t=2)
            op("vector", "tensor_scalar", stat4[:, 56:60], mv[:, :, 1], LN_EPS, None, ALU.add, reads=[k4], writes=[k4])
            op("gpsimd", "tensor_tensor", stat4[:, 56:60], stat4[:, 56:60], mhalf[:, 0:4], ALU.pow, reads=[k4, "mhalf"], writes=[k4])
            op("vector", "scalar_tensor_tensor", stat4[:, 60:64], mv[:, :, 0], -1.0, stat4[:, 56:60], ALU.mult, ALU.mult,
               reads=[k4], writes=[k4])
            return [(stat4[:, 56 + b:57 + b], stat4[:, 60 + b:61 + b]) for b in range(4)], k4

        tr_ctr = [0]
        evac_all_act = True

        def ln_norm(xt, keyx, sidx):
            rstd, nmr, keys = ln_stats(xt, sidx, keyx)
            op("scalar", "activation", xt, xt, AF.Identity, bias=nmr, scale=rstd, reads=[keyx, keys], writes=[keyx])

        def tr_evac(xt, keyx, hT, hkey, b, banks=(0, 1)):
            for half in range(2):
                bank = banks[tr_ctr[0] % 2]
                tr_ctr[0] += 1
                for kk in range(4):
                    k = half * 4 + kk
                    op("tensor", "transpose", psum[:, bank, kk * 128:(kk + 1) * 128], xt[:, k * 128:(k + 1) * 128], ident[:],
                       reads=[keyx, "ident"], writes=[("ps", bank)])
                for kk in range(4):
                    k = half * 4 + kk
                    dst = hT[:, k, b * 128:(b + 1) * 128]
                    src = psum[:, bank, kk * 128:(kk + 1) * 128]
                    if half == 0 and not evac_all_act:
                        op("vector", "tensor_scalar", dst, src, lncol[:, k:k + 1], lncol[:, 8 + k:9 + k], ALU.mult, ALU.add,
                           reads=[("ps", bank), "lncol"], writes=[hkey])
                    else:
                        op("scalar", "activation", dst, src, AF.Identity, bias=lncol[:, 8 + k:9 + k], scale=lncol[:, k:k + 1],
                           reads=[("ps", bank), "lncol"], writes=[hkey])

        R = slice(64, 96)

        def p1_stages(g):
            own = (g % 2 == 1)
            so = g // 2
            hT = hT1[g % 2]
            hkey = ("hT1", g % 2)
            pi = posi[g % 2]
            t0, t1 = tb1[0], tb1[1]
            if own:
                sdst, cdst = sinT[R, so * 512:(so + 1) * 512], cosT[R, so * 512:(so + 1) * 512]
                skey, ckey = ("sinT", so), ("cosT", so)
            else:
                sdst, cdst = t1[R, 0:512], t1[R, 512:1024]
                skey = ckey = "tb1"

            def g1():
                op("vector", "tensor_copy", t0[R, 0:512], pi[R, :], reads=[("posi", g % 2)], writes=["tb0"])
                op("vector", "tensor_scalar", t0[R, 0:512], t0[R, 0:512], ccol[R, 3:4], None, ALU.mult,
                   reads=["tb0", "ccol"], writes=["tb0"])
                op("vector", "tensor_scalar", t0[R, 512:1024], t0[R, 0:512], 0.25, None, ALU.add, reads=["tb0"], writes=["tb0"])
                op("vector", "tensor_copy", tbi[R, :], t0[R, :], reads=["tb0"], writes=["tbi"])
                op("vector", "tensor_copy", t1[R, :], tbi[R, :], reads=["tbi"], writes=["tb1"])
                op("vector", "tensor_tensor", t0[R, :], t0[R, :], t1[R, :], ALU.subtract, reads=["tb0", "tb1"], writes=["tb0"])
                op("vector", "tensor_scalar", t1[R, :], t0[R, :], 0.5, None, ALU.is_gt, reads=["tb0"], writes=["tb1"])
                op("vector", "tensor_tensor", t0[R, :], t0[R, :], t1[R, :], ALU.subtract, reads=["tb0", "tb1"], writes=["tb0"])
                op("vector", "tensor_scalar", t1[R, :], t0[R, :], -0.5, None, ALU.is_lt, reads=["tb0"], writes=["tb1"])
                op("vector", "tensor_tensor", t0[R, :], t0[R, :], t1[R, :], ALU.add, reads=["tb0", "tb1"], writes=["tb0"])
                op("scalar", "activation", sdst, t0[R, 0:512], AF.Sin, scale=ccol[R, 4:5], reads=["tb0", "ccol"], writes=[skey])
                op("scalar", "activation", cdst, t0[R, 512:1024], AF.Sin, scale=ccol[R, 6:7], reads=["tb0", "ccol"], writes=[ckey])
                for j in range(2):
                    for k in range(8):
                        op("tensor", "matmul", psum[0:96, 2 + j, :], wKR[:, k, j * 96:(j + 1) * 96], hT[:, k, :],
                           start=(k == 0), stop=(k == 7), reads=[hkey, ("wKR", k)], writes=[("ps", 2 + j)])
                for k in range(8):
                    op("tensor", "matmul", PS(4), wB[:, k, 256:384], hT[:, k, :], start=(k == 0), stop=(k == 7),
                       reads=[hkey, ("wB", k)], writes=[("ps", 4)])
                if own:
                    for c in range(2):
                        for k in range(8):
                            op("tensor", "matmul", PS(5 + c), wB[:, k, c * 128:(c + 1) * 128], hT[:, k, :], start=(k == 0), stop=(k == 7),
                               reads=[hkey, ("wB", k)], writes=[("ps", 5 + c)])

            def sq_split(c, bank, bcols):
                op("vector", "tensor_scalar", f1[c][:], PS(bank), bcol[:, bcols:bcols + 1], None, ALU.add,
                   reads=[("ps", bank), "bcol"], writes=[("f1", c)])
                op("gpsimd", "tensor_tensor", sq1[c][:], f1[c][:], f1[c][:], ALU.mult, reads=[("f1", c)], writes=[("sq1", c)])
                op("gpsimd", "tensor_copy", sqh[c][:], sq1[c][:], reads=[("sq1", c)], writes=[("sqh", c)])
                op("gpsimd", "tensor_tensor", sql[c][:], sq1[c][:], sqh[c][:], ALU.subtract,
                   reads=[("sq1", c), ("sqh", c)], writes=[("sql", c)])

            def ones_red(n_chunk, norm_n):
                for c in range(n_chunk):
                    op("tensor", "matmul", PS(7), ones_b[:], sqh[c][:], start=(c == 0), stop=False,
                       reads=[("sqh", c), "ones"], writes=[("ps", 7)])
                    op("tensor", "matmul", PS(7), ones_b[:], sql[c][:], start=False, stop=(c == n_chunk - 1),
                       reads=[("sql", c), "ones"], writes=[("ps", 7)])
                op("scalar", "activation", rs1[:], PS(7), AF.Sqrt, bias=epsc[:, 0:1], scale=1.0 / norm_n,
                   reads=[("ps", 7), "epsc"], writes=["rs1"])

            def g2():
                ka, kb_ = kr1[0], kr1[1]
                op("vector", "scalar_tensor_tensor", ka[R, :], psum[R, 2, :], bcol[R, 39:40], cdst, ALU.add, ALU.mult,
                   reads=[("ps", 2), "bcol", ckey], writes=["kr1a"])
                op("vector", "scalar_tensor_tensor", kb_[R, :], psum[R, 3, :], bcol[R, 40:41], sdst, ALU.add, ALU.mult,
                   reads=[("ps", 3), "bcol", skey], writes=["kr1b"])
                op("vector", "tensor_tensor", KT[0][R, g * 512:(g + 1) * 512], ka[R, :], kb_[R, :], ALU.add,
                   reads=["kr1a", "kr1b"], writes=[("KTr0", g)])
                op("gpsimd", "tensor_copy", KT[1][R, g * 512:(g + 1) * 512], KT[0][R, g * 512:(g + 1) * 512],
                   reads=[("KTr0", g)], writes=[("KTr1", g)])
                sq_split(0, 4, 18)

            def g3():
                ones_red(1, 128.0)

            def g4():
                op("vector", "reciprocal", rs1[:], rs1[:], reads=["rs1"], writes=["rs1"])
                op("vector", "scalar_tensor_tensor", ckvT[:, g * 512:(g + 1) * 512], f1[0][:], ccol[:, 2:3], rs1[:],
                   ALU.mult, ALU.mult, reads=[("f1", 0), "rs1", "ccol"], writes=[("ckvT", g)])
                if own:
                    sq_split(0, 5, 16)
                    sq_split(1, 6, 17)

            def g5():
                if own:
                    ones_red(2, 256.0)

            def g6():
                if own:
                    op("vector", "reciprocal", rs1[:], rs1[:], reads=["rs1"], writes=["rs1"])
                    for c in range(2):
                        op("vector", "scalar_tensor_tensor", cqT[:, c, so * 512:(so + 1) * 512], f1[c][:], ccol[:, c:c + 1], rs1[:],
                           ALU.mult, ALU.mult, reads=[("f1", c), "rs1", "ccol"], writes=[("cqT", so, c)])

            return [g1, g2, g3, g4, g5, g6]

        if upto >= 1:
            jobs = [(g, b) for g in range(NG) for b in range(4)]
            due = {}
            SK1 = 2
            for i in range(len(jobs) + 8):
                if i < len(jobs):
                    g, b = jobs[i]
                    if b == 0:
                        op("sync", "dma_start", out=posi[g % 2][64:96, :], in_=pos_d[:, g * 512:(g + 1) * 512],
                           writes=[("posi", g % 2)], dma_slot=f"posi{g % 2}")
                    bi = i % 4
                    r0 = g * 512 + b * 128
                    op("sync", "dma_start", out=xb1[bi][:], in_=x_d[r0:r0 + 128, :], writes=[("xb1", bi)], dma_slot=f"xb1_{bi}")
                    ln_norm(xb1[bi][:], ("xb1", bi), bi)
                if SK1 <= i < len(jobs) + SK1:
                    g, b = jobs[i - SK1]
                    bi = (i - SK1) % 4
                    tr_evac(xb1[bi][:], ("xb1", bi), hT1[g % 2], ("hT1", g % 2), b)
                    if b == 3:
                        for dt_, fn in enumerate(p1_stages(g)):
                            due.setdefault(i + dt_, []).append(fn)
                for fn in due.pop(i, []):
                    fn()
            assert not due


        SC.barrier()
        if upto >= 2:
            for c in range(2):
                load_cast(wuq[:, c, :], w_uq_d[c * 128:(c + 1) * 128, :], 768, ("wuq", c))
                load_cast(wuqs[:, c, :], w_uqs_d[c * 128:(c + 1) * 128, :], 768, ("wuqs", c))
            load_cast(wukv[:], w_ukv_d, 1024, "wukv")
            for vb in range(2):
                op("gpsimd", "memset", Vb[vb][:, :, 64:128], 1.0, writes=[("Vones", vb)])

            SCALE_B = 96.0 ** -0.5
            cv_i = [0]

            def conv_step():
                i = cv_i[0]
                if i >= NCH:
                    return
                cv_i[0] += 1
                kind, c0, nk = CH[i]
                slot = i % 2
                n = nk * 128
                src = WSRC[kind][:, c0:c0 + 128].rearrange("(k p) c -> p k c", p=128)
                op("sync", "dma_start", out=wst[:, slot, 0:n].rearrange("p (k c) -> p k c", c=128), in_=src,
                   writes=[("wst", slot)], dma_slot=f"wst{slot}")
                op("gpsimd", "tensor_copy", cvb[slot][:, 0:n], wst[:, slot, 0:n], reads=[("wst", slot)], writes=[("cvb", slot)])
                op("sync", "dma_start", out=wscr_d[i, :, 0:n], in_=cvb[slot][:, 0:n], reads=[("cvb", slot)],
                   writes=[("wscr", i)], dma_slot=f"cvo{slot}")
            gctr = [0]

            def gen_steps(h):
                buf = h % 2
                steps = []
                for g in range(NG):
                    def kstep(g=g):
                        pb = 6 + (gctr[0] % 2)
                        gctr[0] += 1
                        op("tensor", "matmul", psum[0:64, pb, :], wukv[:, h * 128:h * 128 + 64], ckvT[:, g * 512:(g + 1) * 512],
                           start=True, stop=True, reads=[("ckvT", g), "wukv"], writes=[("ps", pb)])
                        op("vector", "tensor_copy", KT[buf][0:64, g * 512:(g + 1) * 512], psum[0:64, pb, :],
                           reads=[("ps", pb)], writes=[("KT", buf, g)])
                    steps.append(kstep)
                for t8 in range(NT // 8):
                    def vstep(t8=t8):
                        pb = 6 + (gctr[0] % 2)
                        gctr[0] += 1
                        for tt in range(8):
                            t = t8 * 8 + tt
                            op("tensor", "matmul", psum[:, pb, tt * 64:(tt + 1) * 64], ckvT[:, t * 128:(t + 1) * 128],
                               wukv[:, h * 128 + 64:h * 128 + 128], start=True, stop=True,
                               reads=[("ckvT", t // 4), "wukv"], writes=[("ps", pb)])
                        op("vector", "tensor_copy", Vb[buf][:, t8 * 8:(t8 + 1) * 8, 0:64],
                           psum[:, pb, :].rearrange("p (a b) -> p a b", b=64), reads=[("ps", pb)], writes=[("V", buf, t8)])
                    steps.append(vstep)
                return steps

            def gen_q(h, s, qb):
                pq1 = 6 + (gctr[0] % 2)
                pq2 = 6 + ((gctr[0] + 1) % 2)
                gctr[0] += 2
                for c in range(2):
                    op("tensor", "matmul", psum[0:96, pq1, :], wuq[:, c, h * 96:(h + 1) * 96], cqT[:, c, s * 512:(s + 1) * 512],
                       start=(c == 0), stop=(c == 1), reads=[("cqT", s, c), ("wuq", c)], writes=[("ps", pq1)])
                for c in range(2):
                    op("tensor", "matmul", psum[0:96, pq2, :], wuqs[:, c, h * 96:(h + 1) * 96], cqT[:, c, s * 512:(s + 1) * 512],
                       start=(c == 0), stop=(c == 1), reads=[("cqT", s, c), ("wuqs", c)], writes=[("ps", pq2)])
                q = QT[qb]
                op("vector", "tensor_copy", q[0:64, :], psum[0:64, pq1, :], reads=[("ps", pq1)], writes=[("QT", qb)])
                ta, tb_ = qtmp[0], qtmp[1]
                op("vector", "tensor_tensor", ta[R, :], psum[R, pq1, :], cosT[R, s * 512:(s + 1) * 512], ALU.mult,
                   reads=[("ps", pq1), ("cosT", s)], writes=["qta"])
                op("vector", "tensor_tensor", tb_[R, :], psum[R, pq2, :], sinT[R, s * 512:(s + 1) * 512], ALU.mult,
                   reads=[("ps", pq2), ("sinT", s)], writes=["qtb"])
                op("vector", "tensor_tensor", q[R, :], ta[R, :], tb_[R, :], ALU.add, reads=["qta", "qtb"], writes=[("QT", qb)])

            for st_ in gen_steps(0):
                st_()
            qctr = 0
            uctr = 0
            for h in range(8):
                buf = h % 2
                nxt = gen_steps(h + 1) if h < 7 else []
                gen_q(h, 0, qctr % 2)
                for s in range(NP):
                    qb = qctr % 2
                    qctr += 1
                    q = QT[qb]
                    nfull = 8 * s + 4
                    units = []
                    for i in range(nfull // 2):
                        units.append([(2 * i, 0, 512, 0), (2 * i + 1, 0, 512, 512)])
                    t0_ = nfull
                    units.append([(t0_, 0, 512, 0), (t0_ + 1, 128, 384, 512)])
                    units.append([(t0_ + 2, 256, 256, 0), (t0_ + 3, 384, 128, 256)])
                    po = 4 + (s % 2)
                    pend = None
                    first_pv = True

                    for ui, u in enumerate(units):
                        sb_ = uctr % 2
                        pt_i = uctr % 3
                        uctr += 1
                        width = u[-1][3] + u[-1][2]
                        for (t, q0, n, pc) in u:
                            bank = 2 * sb_ + (pc // 512)
                            op("tensor", "matmul", psum[:, bank, (pc % 512):(pc % 512) + n], KT[buf][0:96, t * 128:(t + 1) * 128],
                               q[0:96, q0:q0 + n], start=True, stop=True,
                               reads=[("KT", buf, t // 4), ("KTr%d" % buf, t // 4), ("QT", qb)], writes=[("ps", bank)])
                        sview = psum[:, 2 * sb_:2 * sb_ + 2, :].rearrange("p a b -> p (a b)")[:, 0:width]
                        if u[0][0] < 4:
                            op("scalar", "activation", PT[pt_i][:, 0:width], sview, AF.Exp, bias=FLAG, scale=SCALE_B,
                               reads=[("ps", 2 * sb_), ("ps", 2 * sb_ + 1), "ccol"], writes=[("PT", pt_i)])
                        else:
                            op("scalar", "activation", PT[pt_i][:, 0:width], sview, AF.Exp, scale=SCALE_B,
                               reads=[("ps", 2 * sb_), ("ps", 2 * sb_ + 1)], writes=[("PT", pt_i)])
                        if u[0][0] >= nfull:
                            for (t, q0, n, pc) in u:
                                op("gpsimd", "memset", PT[pt_i][64:128, pc:pc + 64], 0.0, writes=[("PT", pt_i)])
                        if pend is not None:
                            for (t, q0, n, pc) in pend[0]:
                                op("tensor", "matmul", psum[:, po, q0:q0 + n], Vb[buf][:, t, :], PT[pend[1]][:, pc:pc + n],
                                   start=first_pv, stop=False, skip_group_check=True,
                                   reads=[("V", buf, t // 8), ("Vones", buf), ("PT", pend[1])], writes=[("ps", po)])
                                first_pv = False
                        pend = (u, pt_i)
                        if nxt:
                            nxt.pop(0)()
                        if uctr % 4 == 0:
                            conv_step()
                        if ui == 0 and s + 1 < NP:
                            gen_q(h, s + 1, qctr % 2)
                    for (t, q0, n, pc) in pend[0]:
                        op("tensor", "matmul", psum[:, po, q0:q0 + n], Vb[buf][:, t, :], PT[pend[1]][:, pc:pc + n],
                           start=first_pv, stop=False, skip_group_check=True,
                           reads=[("V", buf, t // 8), ("Vones", buf), ("PT", pend[1])], writes=[("ps", po)])
                        first_pv = False
                    rc = rcb[s % 2]
                    c = h // 2
                    lo = (h % 2) * 64
                    op("vector", "reciprocal", rc[0:64, :], psum[64:128, po, :], reads=[("ps", po)], writes=[("rc", s % 2)])
                    op("vector", "tensor_tensor", ybT[lo:lo + 64, c, s * 512:(s + 1) * 512], psum[0:64, po, :], rc[0:64, :], ALU.mult,
                       reads=[("ps", po), ("rc", s % 2)], writes=[("ybT", c, s)])
                while nxt:
                    nxt.pop(0)()
            while cv_i[0] < NCH:
                conv_step()

        stores = []
        if dbg and upto >= 2:
            stores.append(op("sync", "dma_start", out=dbg_yb_d, in_=ybT[:],
                             reads=[("ybT", c, s) for c in range(4) for s in range(NP)], dma_slot="dbg0"))
        SC.barrier()

        a3 = Bump(0)
        BT = a3([128, 8, 640], F32)
        lnbc = a3([128, 4, D], F32)
        xo = a3([128, 4, D], F32)
        xh = [a3([128, D], F32) for _ in range(3)]
        hTe = a3([128, 8, 512], BF16)
        hTw = a3([128, 8, 512], BF16)
        wch = [a3([128, 8, 128], BF16) for _ in range(7)]
        wbufs = list(wch) + [wst[:, sl, hf * 512:(hf + 1) * 512].bitcast(BF16).rearrange("p (k c) -> p k c", c=128)
                             for sl in range(2) for hf in range(2)]
        aqT = [a3([128, 512], BF16) for _ in range(2)]
        akT = [a3([128, 1024], BF16) for _ in range(2)]
        Va = [a3([128, 8, 2, 128], BF16) for _ in range(2)]
        silu2 = [a3([128, 512], F32) for _ in range(2)]
        silu2b = [a3([128, 512], F32) for _ in range(2)]
        zbt = [a3([128, 512], F32) for _ in range(2)]
        tzt = [a3([128, 512], F32) for _ in range(2)]
        Stmp = [a3([128, 512], F32) for _ in range(3)]
        PTa = [a3([128, 512], BF16) for _ in range(4)]
        rca = [a3([128, 512], F32) for _ in range(2)]
        att = [a3([128, 512], F32) for _ in range(2)]
        zaT = a3([128, 4, 512], BF16)
        zbT = a3([128, 4, 512], BF16)
        tat = [a3([128, 512], F32) for _ in range(2)]
        tbt = [a3([128, 512], F32) for _ in range(2)]
        t1t = [a3([128, 512], F32) for _ in range(2)]
        t2t = [a3([128, 512], F32) for _ in range(2)]
        mixT = a3([128, 8, 512], BF16)
        amk = a3([128, 640], F32)

        if upto >= 3:
            op("sync", "dma_start", out=BT[:], in_=bt_d, writes=["BT"], dma_slot="c4")
            op("sync", "dma_start", out=amk[:], in_=amask_d, writes=["amk"], dma_slot="c5")
            op("sync", "dma_start", out=lnbc[:], in_=lnbc_d, writes=["lnbc"], dma_slot="c6")
            for h in range(8):
                op("gpsimd", "tensor_tensor", BT[:, h, :], BT[:, h, :], amk[:], ALU.add, reads=["BT", "amk"], writes=["BT"])
            op("gpsimd", "tensor_scalar", lnbc[:, 0:2, :], lnbc[:, 0:2, :], ALPHA, None, ALU.mult, reads=["lnbc"], writes=["lnbc"])
            for vb in range(2):
                op("gpsimd", "memset", Va[vb][:, :, :, 64:128], 1.0, writes=[("VaOnes", vb)])

            wctr = [0]
            cast_ctr = [0]

            wissued = [0]
            TOTCH = NCH * NP

            def stream_chunk(src_d, c0, nk):
                i = wctr[0]
                wctr[0] += 1
                ci = i % NCH
                assert CH[ci][1] == c0 and CH[ci][2] == nk, (ci, CH[ci], c0, nk)
                while wissued[0] < min(TOTCH, i + len(wbufs)):
                    ii = wissued[0]
                    wissued[0] += 1
                    cj = ii % NCH
                    nkk = CH[cj][2]
                    jb = ii % len(wbufs)
                    op("sync", "dma_start", out=wbufs[jb][:, 0:nkk, :],
                       in_=wscr_d[cj, :, 0:nkk * 128].rearrange("p (k c) -> p k c", c=128),
                       reads=[("wscr", cj)], writes=[("wch", jb)], dma_slot=f"wch{jb}")
                j = i % len(wbufs)
                return wbufs[j], ("wch", j)

            ipb = [0]

            def inproj(wt, wkey, hT, hkey):
                bank = ipb[0] % 2
                ipb[0] += 1
                for k in range(8):
                    op("tensor", "matmul", PS(bank), wt[:, k, :], hT[:, k, :], start=(k == 0), stop=(k == 7),
                       reads=[wkey, hkey], writes=[("ps", bank)])
                return bank

            SCALE_A = 0.125
            JORDER = [3, 4, 2, 5, 1, 6, 0, 7]
            JN = {0: (0, 128), 1: (0, 256), 2: (0, 384), 3: (0, 512), 4: (0, 512), 5: (128, 384), 6: (256, 256), 7: (384, 128)}
            sctr = [0]
            evc = [0]

            def prep_jobs(s):
                e_g, w_g = 2 * s, 2 * s + 1
                return [(g_, hT_, hk_, b) for (g_, hT_, hk_) in ((e_g, hTe, "hTe"), (w_g, hTw, "hTw")) for b in range(4)]

            def prepA(job, idx):
                g_, hT_, hk_, b = job
                xi = idx % 3
                r0 = g_ * 512 + b * 128
                op("sync", "dma_start", out=xh[xi][:], in_=x_d[r0:r0 + 128, :], writes=[("xh", xi)], dma_slot=f"xh{xi}")
                ln_norm(xh[xi][:], ("xh", xi), xi)

            def prepB(job, idx):
                g_, hT_, hk_, b = job
                xi = idx % 3
                tr_evac(xh[xi][:], ("xh", xi), hT_, hk_, b, banks=(2, 3))

            def post_ln(s):
                cols, k4 = ln_stats4(xo, [("xo", b) for b in range(4)])
                for b in range(4):
                    xt = xo[:, b, :]
                    rstd, nmr = cols[b]
                    op("scalar", "activation", xt, xt, AF.Identity, bias=nmr, scale=rstd, reads=[("xo", b), k4], writes=[("xo", b)])
                    op("gpsimd", "tensor_tensor", xt, xt, lnbc[:, 2, :], ALU.mult, reads=[("xo", b), "lnbc"], writes=[("xo", b)])
                    op("gpsimd", "tensor_tensor", xt, xt, lnbc[:, 3, :], ALU.add, reads=[("xo", b), "lnbc"], writes=[("xo", b)])
                    r0 = s * 512 + b * 128
                    stores.append(op("sync", "dma_start", out=out_d[r0:r0 + 128, :], in_=xt, reads=[("xo", b)], dma_slot=f"st{b}"))

            def resid_load(s):
                w_g = 2 * s + 1
                op("sync", "dma_start", out=xo[:], in_=x_d[w_g * 512:(w_g + 1) * 512, :].rearrange("(b p) d -> p b d", p=128),
                   writes=[("xo", b) for b in range(4)], dma_slot="xo")

            def resid_prep(s):
                cols, k4 = ln_stats4(xo, [("xo", b) for b in range(4)])
                for b in range(4):
                    xt = xo[:, b, :]
                    rstd, nmr = cols[b]
                    op("scalar", "activation", xt, xt, AF.Identity, bias=nmr, scale=rstd, reads=[("xo", b), k4], writes=[("xo", b)])
                    op("gpsimd", "tensor_tensor", xt, xt, lnbc[:, 0, :], ALU.mult, reads=[("xo", b), "lnbc"], writes=[("xo", b)])
                    op("gpsimd", "tensor_tensor", xt, xt, lnbc[:, 1, :], ALU.add, reads=[("xo", b), "lnbc"], writes=[("xo", b)])

            xhc = [0]
            fin_defer = []
            hctr = [0]
            if p3cut >= 1:
                j0 = prep_jobs(0)
                for i in range(10):
                    if i < 8:
                        prepA(j0[i], i)
                    if i >= 2:
                        prepB(j0[i - 2], i - 2)
            for s in range(NP):
                if p3cut < 2:
                    continue
                def inproj_jobs(c):
                    cb = c % 2
                    hold = {}

                    def j_ak(hi):
                        def f():
                            if hi == 0:
                                hold["ak"] = stream_chunk(w_in_d, C_AK + c * 128, 8)
                            wt, wkey = hold["ak"]
                            hT_, hk_ = ((hTe, "hTe"), (hTw, "hTw"))[hi]
                            bank = inproj(wt, wkey, hT_, hk_)
                            op("scalar", "activation", akT[cb][:, hi * 512:(hi + 1) * 512], PS(bank), AF.Identity, bias=bcol[:, 4 + c:5 + c],
                               reads=[("ps", bank), "bcol"], writes=[("akT", cb)])
                        return f

                    def j_aq():
                        wt, wkey = stream_chunk(w_in_d, C_AQ + c * 128, 8)
                        bank = inproj(wt, wkey, hTw, "hTw")
                        op("scalar", "activation", aqT[cb][:], PS(bank), AF.Identity, bias=bcol[:, 0 + c:1 + c],
                           reads=[("ps", bank), "bcol"], writes=[("aqT", cb)])

                    def j_av(half, b):
                        def f():
                            if half == 0 and b == 0:
                                hold["av"] = stream_chunk(w_in_d, C_AV + c * 128, 8)
                            wt, wkey = hold["av"]
                            hT_, hk_ = ((hTe, "hTe"), (hTw, "hTw"))[half]
                            bank = 2 + half
                            for k in range(8):
                                op("tensor", "matmul", psum[:, bank, b * 128:(b + 1) * 128], hT_[:, k, b * 128:(b + 1) * 128], wt[:, k, :],
                                   start=(k == 0), stop=(k == 7), reads=[wkey, hk_], writes=[("ps", bank)])
                            if b == 3:
                                op("vector", "tensor_copy", Va[cb][:, half * 4:(half + 1) * 4, :, 0:64],
                                   psum[:, bank, :].rearrange("p (b h d) -> p b h d", h=2, d=64),
                                   reads=[("ps", bank)], writes=[("Va", cb)])
                        return f

                    def j_az():
                        wt, wkey = stream_chunk(w_in_d, C_AZ + c * 128, 8)
                        bank = inproj(wt, wkey, hTw, "hTw")
                        op("scalar", "activation", zbt[cb][:], PS(bank), AF.Identity, bias=bcol[:, 12 + c:13 + c],
                           reads=[("ps", bank), "bcol"], writes=[("zbt", cb)])
                        op("scalar", "activation", tzt[cb][:], zbt[cb][:], AF.Tanh, scale=0.5,
                           reads=[("zbt", cb)], writes=[("tzt", cb)])
                        op("vector", "scalar_tensor_tensor", silu2[cb][:], tzt[cb][:], 1.0, zbt[cb][:], ALU.add, ALU.mult,
                           reads=[("tzt", cb), ("zbt", cb)], writes=[("silu2", cb)])
                    return ([j_ak(0), j_ak(1), j_aq] + [j_av(hf, b) for hf in range(2) for b in range(4)] + [j_az])

                def bgate_jobs():
                    jobs = []
                    for c in range(4):
                        def j_bz(c=c):
                            cb = c % 2
                            wt, wkey = stream_chunk(w_in_d, C_BZ + c * 128, 8)
                            bank = inproj(wt, wkey, hTw, "hTw")
                            op("scalar", "activation", zbt[cb][:], PS(bank), AF.Identity, bias=bcol[:, 19 + c:20 + c],
                               reads=[("ps", bank), "bcol"], writes=[("zbt", cb)])
                            op("scalar", "activation", tzt[cb][:], zbt[cb][:], AF.Tanh, scale=0.5,
                               reads=[("zbt", cb)], writes=[("tzt", cb)])
                            op("vector", "scalar_tensor_tensor", silu2b[cb][:], tzt[cb][:], 1.0, zbt[cb][:], ALU.add, ALU.mult,
                               reads=[("tzt", cb), ("zbt", cb)], writes=[("silu2b", cb)])
                            op("gpsimd", "tensor_tensor", zbT[:, c, :], ybT[:, c, s * 512:(s + 1) * 512], silu2b[cb][:], ALU.mult,
                               reads=[("ybT", c, s), ("silu2b", cb)], writes=[("zbT", c)])
                        jobs.append(j_bz)
                    return jobs

                def attention(c, side_jobs):
                    cb = c % 2
                    steps = []
                    for hh in range(2):
                        po = 6 + (hctr[0] % 2)
                        hctr[0] += 1
                        for ji, j in enumerate(JORDER):
                            steps.append((hh, po, j, ji == 0, ji == 7))

                    def qk_stage(st):
                        hh, po, j, first, last = st
                        h = 2 * c + hh
                        lo = hh * 64
                        q0, n = JN[j]
                        u0 = q0 - (j - 4) * 128
                        sbk = 4 + (sctr[0] % 2)
                        si = sctr[0] % 3
                        pi_ = sctr[0] % 4
                        sctr[0] += 1
                        op("tensor", "matmul", psum[:, sbk, 0:n], akT[cb][lo:lo + 64, j * 128:(j + 1) * 128], aqT[cb][lo:lo + 64, q0:q0 + n],
                           start=True, stop=True, reads=[("akT", cb), ("aqT", cb)], writes=[("ps", sbk)])
                        op("vector", "scalar_tensor_tensor", Stmp[si][:, 0:n], psum[:, sbk, 0:n], SCALE_A, BT[:, h, u0:u0 + n], ALU.mult, ALU.add,
                           reads=[("ps", sbk), "BT"], writes=[("Stmp", si)])
                        if s == 0 and j < 4:
                            op("scalar", "activation", PTa[pi_][:, 0:n], Stmp[si][:, 0:n], AF.Exp, bias=FLAG,
                               reads=[("Stmp", si), "ccol"], writes=[("PTa", pi_)])
                        else:
                            op("scalar", "activation", PTa[pi_][:, 0:n], Stmp[si][:, 0:n], AF.Exp,
                               reads=[("Stmp", si)], writes=[("PTa", pi_)])
                        if fin_defer:
                            fin_defer.pop(0)()
                        return pi_

                    def pv_stage(st, pi_):
                        hh, po, j, first, last = st
                        h = 2 * c + hh
                        lo = hh * 64
                        q0, n = JN[j]
                        op("tensor", "matmul", psum[:, po, q0:q0 + n], Va[cb][:, j, hh, :], PTa[pi_][:, 0:n],
                           start=first, stop=False, skip_group_check=True,
                           reads=[("Va", cb), ("VaOnes", cb), ("PTa", pi_)], writes=[("ps", po)])
                        if last:
                            ri = h % 2
                            for pc_ in range(4):
                                def fin(pc_=pc_, ri=ri, lo=lo, po=po, c=c, cb=cb):
                                    cs_ = slice(pc_ * 128, (pc_ + 1) * 128)
                                    op("scalar", "activation", rca[ri][0:64, cs_], psum[64:128, po, cs_], AF.Ln, reads=[("ps", po)], writes=[("rca", ri)])
                                    op("scalar", "activation", rca[ri][0:64, cs_], rca[ri][0:64, cs_], AF.Exp, scale=-1.0, reads=[("rca", ri)], writes=[("rca", ri)])
                                    op("vector", "tensor_tensor", att[ri][lo:lo + 64, cs_], psum[0:64, po, cs_], rca[ri][0:64, cs_], ALU.mult,
                                       reads=[("ps", po), ("rca", ri)], writes=[("att", ri)])
                                    op("vector", "scalar_tensor_tensor", zaT[lo:lo + 64, c, cs_], att[ri][lo:lo + 64, cs_], bcol[lo:lo + 64, 8 + c:9 + c],
                                       silu2[cb][lo:lo + 64, cs_], ALU.add, ALU.mult,
                                       reads=[("att", ri), "bcol", ("silu2", cb)], writes=[("zaT", c)])
                                fin_defer.append(fin)

                    LOOK = 2
                    pis = {}
                    side = list(side_jobs)
                    every = 1 if len(side) > 8 else max(1, (len(steps) - 2) // max(1, len(side)))
                    for i in range(len(steps) + LOOK):
                        if i < len(steps):
                            pis[i] = qk_stage(steps[i])
                        if i - LOOK >= 0:
                            pv_stage(steps[i - LOOK], pis[i - LOOK])
                        if side and i >= 1 and (i - 1) % every == 0:
                            side.pop(0)()
                    while side:
                        side.pop(0)()

                for jb in inproj_jobs(0):
                    jb()
                if s > 0 and p3cut >= 6:
                    post_ln(s - 1)
                resid_load(s)
                for c in range(4):
                    attention(c, inproj_jobs(c + 1) if c < 3 else (bgate_jobs() if p3cut >= 3 else []))
                while fin_defer:
                    fin_defer.pop(0)()
                if p3cut < 3:
                    continue
                resid_prep(s)

                if p3cut < 4:
                    continue
                for oc in range(8):
                    ob = oc % 2
                    wt, wkey = stream_chunk(w_in_d, C_GA + oc * 128, 8)
                    bank = inproj(wt, wkey, hTw, "hTw")
                    op("scalar", "activation", tat[ob][:], PS(bank), AF.Tanh, bias=bcolh[:, 23 + oc:24 + oc], scale=0.5,
                       reads=[("ps", bank), "bcolh"], writes=[("tat", ob)])
                    wt, wkey = stream_chunk(w_pa_d, oc * 128, 4)
                    for c in range(4):
                        op("tensor", "matmul", PS(7), wt[:, c, :], zaT[:, c, :], start=(c == 0), stop=(c == 3),
                           reads=[wkey, ("zaT", c)], writes=[("ps", 7)])
                    op("vector", "scalar_tensor_tensor", t1t[ob][:], tat[ob][:], 1.0, PS(7), ALU.add, ALU.mult,
                       reads=[("tat", ob), ("ps", 7)], writes=[("t1t", ob)])
                    wt, wkey = stream_chunk(w_in_d, C_GB + oc * 128, 8)
                    bank = inproj(wt, wkey, hTw, "hTw")
                    op("scalar", "activation", tbt[ob][:], PS(bank), AF.Tanh, bias=bcolh[:, 31 + oc:32 + oc], scale=0.5,
                       reads=[("ps", bank), "bcolh"], writes=[("tbt", ob)])
                    wt, wkey = stream_chunk(w_pb_d, oc * 128, 4)
                    for c in range(4):
                        op("tensor", "matmul", PS(6), wt[:, c, :], zbT[:, c, :], start=(c == 0), stop=(c == 3),
                           reads=[wkey, ("zbT", c)], writes=[("ps", 6)])
                    op("vector", "scalar_tensor_tensor", t2t[ob][:], tbt[ob][:], 1.0, PS(6), ALU.add, ALU.mult,
                       reads=[("tbt", ob), ("ps", 6)], writes=[("t2t", ob)])
                    op("gpsimd", "tensor_tensor", mixT[:, oc, :], t1t[ob][:], t2t[ob][:], ALU.add,
                       reads=[("t1t", ob), ("t2t", ob)], writes=[("mixT", oc)])
                if p3cut < 5:
                    continue
                pj = prep_jobs(s + 1) if s + 1 < NP else []
                if pj:
                    prepA(pj[0], 0)
                    prepA(pj[1], 1)
                for cc in range(8):
                    if pj and cc + 2 < 8:
                        prepA(pj[cc + 2], cc + 2)
                    if pj:
                        prepB(pj[cc], cc)
                    wt, wkey = stream_chunk(w_out_d, cc * 128, 8)
                    bank = 4 + (cc % 2)
                    for b in range(4):
                        for oc in range(8):
                            op("tensor", "matmul", psum[:, bank, b * 128:(b + 1) * 128], mixT[:, oc, b * 128:(b + 1) * 128], wt[:, oc, :],
                               start=(oc == 0), stop=(oc == 7), reads=[wkey, ("mixT", oc)], writes=[("ps", bank)])
                    op("vector", "scalar_tensor_tensor", xo[:, :, cc * 128:(cc + 1) * 128],
                       psum[:, bank, :].rearrange("p (b d) -> p b d", d=128), 0.25, xo[:, :, cc * 128:(cc + 1) * 128], ALU.mult, ALU.add,
                       reads=[("ps", bank)] + [("xo", b) for b in range(4)], writes=[("xo", b) for b in range(4)])
                if p3cut < 6:
                    continue
                if s == NP - 1:
                    post_ln(s)


        SC.emit(nc, es, final_waits=stores)
    return nc


def host_consts():
    c = {}
    c["ident"] = np.eye(128, dtype=np.float32)
    k = np.arange(128)[:, None]
    u = np.arange(640)[None, :]
    kc = k // 64
    uc = u // 64
    vis = (kc <= uc) & (kc >= uc - 8)
    c["amask"] = np.where(vis, 0.0, NEG).astype(np.float32)
    c["bt_idx"] = (np.clip(u - k, -128, 128) + 128).astype(np.int64)
    return c


def core_inputs(inp, b, p, S=SEQ, consts=None):
    cs = consts or host_consts()
    f32 = np.float32
    x = np.asarray(inp["x"][b], dtype=f32)
    pos = np.asarray(inp["positions"][b], dtype=np.int32)
    if p == 0:
        xl = np.concatenate([np.zeros((512, D), f32), x[:S - 512]], axis=0)
        pl = np.concatenate([np.zeros((512,), np.int32), pos[:S - 512]], axis=0)
    else:
        xl, pl = x[:S], pos[:S]
    b_in = np.asarray(inp["b_in"][0], f32)
    w_in = np.ascontiguousarray(np.asarray(inp["w_in"][0], f32))
    starts = ([C_AQ + 128 * i for i in range(4)] + [C_AK + 128 * i for i in range(4)] + [C_AV + 128 * i for i in range(4)]
              + [C_AZ + 128 * i for i in range(4)] + [C_CQ, C_CQ + 128, C_CKV] + [C_BZ + 128 * i for i in range(4)]
              + [C_GA + 128 * i for i in range(8)] + [C_GB + 128 * i for i in range(8)])
    bcol = np.zeros((128, 44), f32)
    for i, st in enumerate(starts):
        bcol[:, i] = b_in[st:st + 128]
    bkr = b_in[C_KR:C_KR + 32]
    bcol[64:96, 39] = bkr
    bcol[64:80, 40] = bkr[16:32]
    bcol[80:96, 40] = bkr[0:16]
    ccol = np.zeros((128, 8), f32)
    qg = np.asarray(inp["q_norm_g"][0], f32)
    ccol[:, 0] = qg[0:128]
    ccol[:, 1] = qg[128:256]
    ccol[:, 2] = np.asarray(inp["kv_norm_g"][0], f32)
    inv_freq = (10000.0 ** (-np.arange(16, dtype=np.float32) / 16)).astype(f32)
    r = np.arange(32)
    ccol[64:96, 3] = (inv_freq[r % 16].astype(np.float64) / TWO_PI).astype(f32)
    tp = np.float32(6.28318)
    ccol[64:80, 4] = -tp
    ccol[80:96, 4] = tp
    ccol[64:96, 6] = tp
    ccol[:, 5] = NEG if p == 0 else 0.0
    lncol = np.zeros((128, 16), f32)
    lncol[:, 0:8] = np.asarray(inp["ln_in_g"], f32).reshape(8, 128).T
    lncol[:, 8:16] = np.asarray(inp["ln_in_b"], f32).reshape(8, 128).T
    lnbc = np.stack([np.broadcast_to(np.asarray(v, f32).reshape(1, D), (128, D)) for v in
                     (inp["ln_in_g"], inp["ln_in_b"], inp["ln_post_g"][0], inp["ln_post_b"][0])], axis=1)
    rb = np.asarray(inp["rel_bias"][0], f32)
    bt = np.ascontiguousarray(np.transpose(rb[cs["bt_idx"]], (0, 2, 1)))
    w_krp = np.zeros((D, 192), f32)
    w_krp[:, 64:96] = w_in[:, C_KR:C_KR + 32]
    w_krp[:, 160:176] = w_in[:, C_KR + 16:C_KR + 32]
    w_krp[:, 176:192] = w_in[:, C_KR:C_KR + 16]
    w_uq = np.ascontiguousarray(np.asarray(inp["w_uq"][0], f32))
    w_uqs = np.zeros_like(w_uq)
    for h in range(8):
        w_uqs[:, h * 96 + 64:h * 96 + 80] = w_uq[:, h * 96 + 80:h * 96 + 96]
        w_uqs[:, h * 96 + 80:h * 96 + 96] = w_uq[:, h * 96 + 64:h * 96 + 80]
    return {
        "x": np.ascontiguousarray(xl), "pos32": np.ascontiguousarray(np.broadcast_to(pl[None, :], (32, S))),
        "ident": cs["ident"], "bcol": bcol, "ccol": ccol, "lncol": lncol, "lnbc": np.ascontiguousarray(lnbc),
        "bt": bt, "amask": cs["amask"], "w_in": w_in, "w_krp": w_krp, "w_uq": w_uq, "w_uq_sw": w_uqs,
        "w_ukv": np.ascontiguousarray(np.asarray(inp["w_ukv"][0], f32)),
        "w_proj_a": np.ascontiguousarray(np.asarray(inp["w_proj_a"][0], f32)),
        "w_proj_b": np.ascontiguousarray(np.asarray(inp["w_proj_b"][0], f32)),
        "w_out": np.ascontiguousarray(np.asarray(inp["w_out"][0], f32)),
    }


def assemble(results, n_batch, S=SEQ):
    out = np.zeros((n_batch, S, D), np.float32)
    for b in range(n_batch):
        for p in range(2):
            o = np.asarray(results[b * 2 + p]["out"], np.float32)
            for s in range(S // 1024):
                gg = 2 * s + p
                out[b, gg * 512:(gg + 1) * 512, :] = o[s * 512:(s + 1) * 512, :]
    return out


_NC_CACHE = {}


def kernel(**inputs):
    inp = {k: np.asarray(v) for k, v in inputs.items()}
    nb = inp["x"].shape[0]
    S = inp["x"].shape[1]
    if S not in _NC_CACHE:
        _NC_CACHE[S] = build_nc(S)
    nc = _NC_CACHE[S]
    cs = host_consts()
    in_maps = [core_inputs(inp, b, p, S, cs) for b in range(nb) for p in range(2)]
    res = run_bass_kernel_spmd(nc, in_maps, core_ids=list(range(2 * nb)))
    return assemble(res.results, nb, S)
```

```python
import math
from contextlib import ExitStack

import numpy as np
import concourse.bass as bass
import concourse.mybir as mybir
from concourse.bass_utils import run_bass_kernel_spmd

F32 = mybir.dt.float32
BF16 = mybir.dt.bfloat16
I32 = mybir.dt.int32
AF = mybir.ActivationFunctionType
ALU = mybir.AluOpType

D = 1024
SEQ = 8192
BATCH = 4
IN_COLS = 5024
C_AQ, C_AK, C_AV, C_AZ, C_CQ, C_CKV, C_KR, C_BZ, C_GA, C_GB = 0, 512, 1024, 1536, 2048, 2304, 2432, 2464, 2976, 4000
ALPHA = 2.0 ** 0.25
LN_EPS = 1e-5
RMS_EPS = 1e-6
NEG = -30000.0
TWO_PI = 2.0 * math.pi

ENGS = ("tensor", "vector", "scalar", "gpsimd", "sync")
SEM_EPOCH = 30000


class Op:
    __slots__ = ("eng", "name", "args", "kw", "deps", "signal", "idx", "dma", "dsem", "dval")

    def __init__(self, eng, name, args, kw, dma):
        self.eng = eng
        self.name = name
        self.args = args
        self.kw = kw
        self.deps = []
        self.signal = False
        self.idx = -1
        self.dma = dma
        self.dsem = None
        self.dval = 0


class Sched:
    def __init__(self):
        self.q = {e: [] for e in ENGS}
        self.last_w = {}
        self.readers = {}
        self.dma_slots = {}
        self.slot_names = []
        self.all_dma = []
        self.pending = {e: [] for e in ENGS}
        self.ps_read = {}

    def op(self, eng, name, *args, reads=(), writes=(), dma_slot=None, **kw):
        o = Op(eng, name, args, kw, dma_slot is not None)
        o.idx = len(self.q[eng])
        deps = list(self.pending[eng])
        self.pending[eng] = []
        ex = [r for r in reads if isinstance(r, tuple) and r[0] == "ps"]
        if ex:
            reads = [r for r in reads if not (isinstance(r, tuple) and r[0] == "ps")]
            for r in ex:
                w = self.last_w.get(r)
                if w is not None and not (w.eng == eng and self.ps_read.get(r, False) and not w.dma):
                    deps.append(w)
                deps.extend(self.readers.get(r, ()))
                self.last_w[r] = o
                self.readers[r] = []
                self.ps_read[r] = True
        for r in writes:
            if isinstance(r, tuple) and r[0] == "ps":
                self.ps_read[r] = False
        for r in reads:
            w = self.last_w.get(r)
            if w is not None:
                deps.append(w)
        for r in writes:
            w = self.last_w.get(r)
            if w is not None:
                deps.append(w)
            deps.extend(self.readers.get(r, ()))
        for r in reads:
            lst = self.readers.setdefault(r, [])
            if not o.dma:
                lst[:] = [x for x in lst if x.dma or x.eng != eng]
            lst.append(o)
        for r in writes:
            self.last_w[r] = o
            self.readers[r] = []
        seen = set()
        for d in deps:
            if d is o or id(d) in seen:
                continue
            seen.add(id(d))
            if (not d.dma) and d.eng == eng and eng == "tensor":
                continue
            o.deps.append(d)
            d.signal = True
        if dma_slot is not None:
            if dma_slot not in self.dma_slots:
                self.dma_slots[dma_slot] = 0
                self.slot_names.append(dma_slot)
            self.dma_slots[dma_slot] += 16
            o.dsem = dma_slot
            o.dval = self.dma_slots[dma_slot]
            self.all_dma.append(o)
        self.q[eng].append(o)
        return o

    def barrier(self):
        lasts = []
        for e in ENGS:
            comp = [o for o in self.q[e] if not o.dma]
            if comp:
                lasts.append(comp[-1])
        last_dma = {}
        for o in self.all_dma:
            last_dma[o.dsem] = o
        lasts.extend(last_dma.values())
        for e in ENGS:
            self.pending[e] = list(self.pending[e]) + lasts

    def emit(self, nc, es, final_waits=()):
        n_sig = {e: sum(1 for o in self.q[e] if o.signal and not o.dma) for e in ENGS}
        esems = {}
        for e in ENGS:
            n_ep = n_sig[e] // SEM_EPOCH + 1
            esems[e] = [es.enter_context(nc.semaphore(f"s_{e}_{k}")) for k in range(n_ep)]
        dsems = {}
        for i, s in enumerate(self.slot_names):
            dsems[s] = es.enter_context(nc.semaphore(f"d_{i}"))
        for e in ENGS:
            c = 0
            for o in self.q[e]:
                if (not o.dma) and o.signal:
                    c += 1
                    o.dval = c

        def ev_of(d):
            if d.dma:
                return dsems[d.dsem], d.dval, None, 0
            ep = (d.dval - 1) // SEM_EPOCH
            return esems[d.eng][ep], d.dval - ep * SEM_EPOCH, d.eng, ep

        block = es.enter_context(nc.Block())
        sched = self

        def make(e):
            def body(eng):
                waited = {}
                max_ep = {}
                for o in sched.q[e]:
                    for d in o.deps:
                        sem, val, deng, ep = ev_of(d)
                        if deng is not None and max_ep.get(deng, -1) > ep:
                            continue
                        if waited.get(sem.name, 0) >= val:
                            continue
                        waited[sem.name] = val
                        if deng is not None:
                            max_ep[deng] = max(max_ep.get(deng, -1), ep)
                        eng.wait_ge(sem, val)
                    ins = getattr(eng, o.name)(*o.args, **o.kw)
                    if o.dma:
                        ins.then_inc(dsems[o.dsem], 16)
                    elif o.signal:
                        ep = (o.dval - 1) // SEM_EPOCH
                        ins.then_inc(esems[e][ep], 1)
                if e == "sync":
                    for d in final_waits:
                        sem, val, _, _ = ev_of(d)
                        eng.wait_ge(sem, val)
            return body

        block.tensor(make("tensor"))
        block.vector(make("vector"))
        block.scalar(make("scalar"))
        block.gpsimd(make("gpsimd"))
        block.sync(make("sync"))


def build_nc(S=SEQ, dbg=False, upto=3, p1cut=9, p3cut=9):
    assert S % 1024 == 0
    NG = S // 512
    NP = NG // 2
    SO = S // 2
    NT = S // 128

    nc = bass.Bass("TRN2", target_bir_lowering=False, dynamic_dma_scratch_size=1024)

    def dram(n, sh, dt, kind="ExternalInput"):
        return nc.dram_tensor(n, sh, dt, kind=kind).ap()

    x_d = dram("x", [S, D], F32)
    pos_d = dram("pos32", [32, S], I32)
    ident_d = dram("ident", [128, 128], F32)
    bcol_d = dram("bcol", [128, 44], F32)
    ccol_d = dram("ccol", [128, 8], F32)
    lncol_d = dram("lncol", [128, 16], F32)
    lnbc_d = dram("lnbc", [128, 4, D], F32)
    bt_d = dram("bt", [128, 8, 640], F32)
    amask_d = dram("amask", [128, 640], F32)
    w_in_d = dram("w_in", [D, IN_COLS], F32)
    w_krp_d = dram("w_krp", [D, 192], F32)
    w_uq_d = dram("w_uq", [256, 768], F32)
    w_uqs_d = dram("w_uq_sw", [256, 768], F32)
    w_ukv_d = dram("w_ukv", [128, 1024], F32)
    w_pa_d = dram("w_proj_a", [512, D], F32)
    w_pb_d = dram("w_proj_b", [512, D], F32)
    w_out_d = dram("w_out", [D, D], F32)
    out_d = dram("out", [SO, D], F32, kind="ExternalOutput")
    CH = []
    for c in range(4):
        CH += [("in", C_AK + c * 128, 8), ("in", C_AQ + c * 128, 8), ("in", C_AV + c * 128, 8), ("in", C_AZ + c * 128, 8)]
    for c in range(4):
        CH.append(("in", C_BZ + c * 128, 8))
    for oc in range(8):
        CH += [("in", C_GA + oc * 128, 8), ("pa", oc * 128, 4), ("in", C_GB + oc * 128, 8), ("pb", oc * 128, 4)]
    for cc in range(8):
        CH.append(("out", cc * 128, 8))
    NCH = len(CH)
    wscr_d = dram("wscr", [NCH, 128, 1024], BF16, kind="Internal")
    WSRC = {"in": w_in_d, "pa": w_pa_d, "pb": w_pb_d, "out": w_out_d}
    if dbg:
        dbg_yb_d = dram("dbg_yb", [128, 4, SO], BF16, kind="ExternalOutput")

    SC = Sched()
    op = SC.op
    es = ExitStack()
    with es:
        def sbt(n, sh, dt):
            return es.enter_context(nc.sbuf_tensor(n, sh, dt))

        ident = sbt("ident_s", [128, 128], F32)
        ones_b = sbt("ones_b", [128, 128], BF16)
        bcol = sbt("bcol_s", [128, 44], F32)
        bcolh = sbt("bcolh_s", [128, 44], F32)
        ccol = sbt("ccol_s", [128, 8], F32)
        lncol = sbt("lncol_s", [128, 16], F32)
        mhalf = sbt("mhalf_s", [128, 8], F32)
        epsc = sbt("epsc_s", [128, 2], F32)
        stat = sbt("stat_s", [128, 96], F32)
        stat4 = sbt("stat4_s", [128, 64], F32)
        wst = sbt("wst_s", [128, 2, 1024], F32)
        ybT = sbt("ybT_s", [128, 4, SO], BF16)
        psum = es.enter_context(nc.psum_tensor("ps", [128, 8, 512], F32))

        P3 = 180 * 1024
        P12 = 12 * S + 72 * 1024
        ARENA = max(P12, P3)
        arena = sbt("arena", [128, ARENA // 2], BF16)

        def view(off, shape, dt):
            n = 1
            for s_ in shape[1:]:
                n *= s_
            esz = 4 if dt in (F32, I32) else 2
            assert off % 4 == 0 and off + n * esz <= ARENA, (off, shape, ARENA)
            v = arena[:, off // 2: off // 2 + n * esz // 2]
            if esz == 4:
                v = v.bitcast(dt)
            if len(shape) == 3:
                v = v.rearrange("p (a b) -> p a b", b=shape[2])
            elif len(shape) == 4:
                v = v.rearrange("p (a b c) -> p a b c", b=shape[2], c=shape[3])
            return v

        class Bump:
            def __init__(self, off=0):
                self.off = off

            def __call__(self, shape, dt):
                n = 1
                for s_ in shape[1:]:
                    n *= s_
                esz = 4 if dt in (F32, I32) else 2
                v = view(self.off, shape, dt)
                self.off += (n * esz + 31) // 32 * 32
                return v

        al = Bump(0)
        ckvT = al([128, S], BF16)
        cqT = al([128, 2, SO], BF16)
        KT = [al([128, S], BF16), al([128, S], BF16)]
        cosT = al([128, SO], F32)
        sinT = al([128, SO], F32)
        p2_base = al.off
        Vb = [al([128, NT, 128], BF16), al([128, NT, 128], BF16)]
        QT = [al([128, 512], BF16), al([128, 512], BF16)]
        PT = [al([128, 1024], BF16) for _ in range(3)]
        rcb = [al([128, 512], F32) for _ in range(2)]
        qtmp = [al([128, 512], F32) for _ in range(2)]
        wuq = al([128, 2, 768], BF16)
        wuqs = al([128, 2, 768], BF16)
        wukv = al([128, 1024], BF16)
        cvb = [al([128, 1024], BF16) for _ in range(2)]
        a1 = Bump(p2_base)
        xb1 = [a1([128, D], F32) for _ in range(5)]
        hT1 = [a1([128, 8, 512], BF16) for _ in range(2)]
        wB = a1([128, 8, 384], BF16)
        wKR = a1([128, 8, 192], BF16)
        f1 = [a1([128, 512], F32) for _ in range(2)]
        sq1 = [a1([128, 512], F32) for _ in range(2)]
        rs1 = a1([128, 512], F32)
        sqh = [a1([128, 512], BF16) for _ in range(2)]
        sql = [a1([128, 512], BF16) for _ in range(2)]
        tb1 = [a1([128, 1024], F32) for _ in range(2)]
        tbi = a1([128, 1024], I32)
        posi = [a1([128, 512], I32) for _ in range(2)]
        kr1 = [a1([128, 512], F32) for _ in range(2)]

        dcount = [0]

        def PS(b):
            return psum[:, b, :]

        def load_cast(dst, src_ap, n_elem, key, eng_cast="gpsimd"):
            slot = dcount[0] % 2
            dcount[0] += 1
            st = wst[:, slot, 0:n_elem]
            shp = dst.shape
            if len(shp) == 3:
                st = st.rearrange("p (a b) -> p a b", b=shp[2])
            op("sync", "dma_start", out=st, in_=src_ap, writes=[("wst", slot)], dma_slot=f"wst{slot}")
            if eng_cast == "scalar":
                op("scalar", "activation", dst, st, AF.Copy, reads=[("wst", slot)], writes=[key])
            else:
                op(eng_cast, "tensor_copy", dst, st, reads=[("wst", slot)], writes=[key])

        op("sync", "dma_start", out=ident[:], in_=ident_d, writes=["ident"], dma_slot="c0")
        op("sync", "dma_start", out=bcol[:], in_=bcol_d, writes=["bcol"], dma_slot="c1")
        op("sync", "dma_start", out=ccol[:], in_=ccol_d, writes=["ccol"], dma_slot="c2")
        op("sync", "dma_start", out=lncol[:], in_=lncol_d, writes=["lncol"], dma_slot="c3")
        op("gpsimd", "memset", ones_b[:], 1.0, writes=["ones"])
        op("gpsimd", "memset", mhalf[:], -0.5, writes=["mhalf"])
        op("gpsimd", "memset", epsc[:], RMS_EPS, writes=["epsc"])
        op("gpsimd", "tensor_scalar", bcolh[:], bcol[:], 0.5, None, ALU.mult, reads=["bcol"], writes=["bcolh"])
        for k in range(8):
            load_cast(wB[:, k, :], w_in_d[k * 128:(k + 1) * 128, C_CQ:C_CQ + 384], 384, ("wB", k))
            load_cast(wKR[:, k, :], w_krp_d[k * 128:(k + 1) * 128, :], 192, ("wKR", k))

        ZC = ccol[:, 7:8]
        FLAG = ccol[:, 5:6]

        def ln_stats(xt, sidx, keyx):
            keys = ("st", sidx)
            b = sidx * 16
            st6 = stat[:, b:b + 12].rearrange("p (a b) -> p a b", b=6)
            op("vector", "bn_stats", st6[:, 0, :], xt[:, 0:512], reads=[keyx], writes=[keys])
            op("vector", "bn_stats", st6[:, 1, :], xt[:, 512:1024], reads=[keyx], writes=[keys])
            op("vector", "bn_aggr", stat[:, b + 12:b + 14], stat[:, b:b + 12], reads=[keys], writes=[keys])
            op("vector", "tensor_scalar", stat[:, b + 13:b + 14], stat[:, b + 13:b + 14], LN_EPS, None, ALU.add,
               reads=[keys], writes=[keys])
            op("gpsimd", "tensor_tensor", stat[:, b + 13:b + 14], stat[:, b + 13:b + 14], mhalf[:, 0:1], ALU.pow,
               reads=[keys, "mhalf"], writes=[keys])
            op("vector", "scalar_tensor_tensor", stat[:, b + 14:b + 15], stat[:, b + 12:b + 13], -1.0,
               stat[:, b + 13:b + 14], ALU.mult, ALU.mult, reads=[keys], writes=[keys])
            return stat[:, b + 13:b + 14], stat[:, b + 14:b + 15], keys

        def ln_stats4(x4, keysx):
            k4 = "st4"
            for b in range(4):
                st6 = stat4[:, b * 12:(b + 1) * 12].rearrange("p (a b) -> p a b", b=6)
                op("vector", "bn_stats", st6[:, 0, :], x4[:, b, 0:512], reads=[keysx[b]], writes=[k4])
                op("vector", "bn_stats", st6[:, 1, :], x4[:, b, 512:1024], reads=[keysx[b]], writes=[k4])
                op("vector", "bn_aggr", stat4[:, 48 + 2 * b:50 + 2 * b], stat4[:, b * 12:(b + 1) * 12], reads=[k4], writes=[k4])
            mv = stat4[:, 48:56].rearrange("p (b t) -> p b t", t=2)
            op("vector", "tensor_scalar", stat4[:, 56:60], mv[:, :, 1], LN_EPS, None, ALU.add, reads=[k4], writes=[k4])
            op("gpsimd", "tensor_tensor", stat4[:, 56:60], stat4[:, 56:60], mhalf[:, 0:4], ALU.pow, reads=[k4, "mhalf"], writes=[k4])
            op("vector", "scalar_tensor_tensor", stat4[:, 60:64], mv[:, :, 0], -1.0, stat4[:, 56:60], ALU.mult, ALU.mult,
               reads=[k4], writes=[k4])
            return [(stat4[:, 56 + b:57 + b], stat4[:, 60 + b:61 + b]) for b in range(4)], k4

        tr_ctr = [0]
        evac_all_act = True

        def ln_norm(xt, keyx, sidx):
            rstd, nmr, keys = ln_stats(xt, sidx, keyx)
            op("scalar", "activation", xt, xt, AF.Identity, bias=nmr, scale=rstd, reads=[keyx, keys], writes=[keyx])

        def tr_evac(xt, keyx, hT, hkey, b, banks=(0, 1)):
            for half in range(2):
                bank = banks[tr_ctr[0] % 2]
                tr_ctr[0] += 1
                for kk in range(4):
                    k = half * 4 + kk
                    op("tensor", "transpose", psum[:, bank, kk * 128:(kk + 1) * 128], xt[:, k * 128:(k + 1) * 128], ident[:],
                       reads=[keyx, "ident"], writes=[("ps", bank)])
                for kk in range(4):
                    k = half * 4 + kk
                    dst = hT[:, k, b * 128:(b + 1) * 128]
                    src = psum[:, bank, kk * 128:(kk + 1) * 128]
                    if half == 0 and not evac_all_act:
                        op("vector", "tensor_scalar", dst, src, lncol[:, k:k + 1], lncol[:, 8 + k:9 + k], ALU.mult, ALU.add,
                           reads=[("ps", bank), "lncol"], writes=[hkey])
                    else:
                        op("scalar", "activation", dst, src, AF.Identity, bias=lncol[:, 8 + k:9 + k], scale=lncol[:, k:k + 1],
                           reads=[("ps", bank), "lncol"], writes=[hkey])

        R = slice(64, 96)

        def p1_stages(g):
            own = (g % 2 == 1)
            so = g // 2
            hT = hT1[g % 2]
            hkey = ("hT1", g % 2)
            pi = posi[g % 2]
            t0, t1 = tb1[0], tb1[1]
            if own:
                sdst, cdst = sinT[R, so * 512:(so + 1) * 512], cosT[R, so * 512:(so + 1) * 512]
                skey, ckey = ("sinT", so), ("cosT", so)
            else:
                sdst, cdst = t1[R, 0:512], t1[R, 512:1024]
                skey = ckey = "tb1"

            def g1():
                op("vector", "tensor_copy", t0[R, 0:512], pi[R, :], reads=[("posi", g % 2)], writes=["tb0"])
                op("vector", "tensor_scalar", t0[R, 0:512], t0[R, 0:512], ccol[R, 3:4], None, ALU.mult,
                   reads=["tb0", "ccol"], writes=["tb0"])
                op("vector", "tensor_scalar", t0[R, 512:1024], t0[R, 0:512], 0.25, None, ALU.add, reads=["tb0"], writes=["tb0"])
                op("vector", "tensor_copy", tbi[R, :], t0[R, :], reads=["tb0"], writes=["tbi"])
                op("vector", "tensor_copy", t1[R, :], tbi[R, :], reads=["tbi"], writes=["tb1"])
                op("vector", "tensor_tensor", t0[R, :], t0[R, :], t1[R, :], ALU.subtract, reads=["tb0", "tb1"], writes=["tb0"])
                op("vector", "tensor_scalar", t1[R, :], t0[R, :], 0.5, None, ALU.is_gt, reads=["tb0"], writes=["tb1"])
                op("vector", "tensor_tensor", t0[R, :], t0[R, :], t1[R, :], ALU.subtract, reads=["tb0", "tb1"], writes=["tb0"])
                op("vector", "tensor_scalar", t1[R, :], t0[R, :], -0.5, None, ALU.is_lt, reads=["tb0"], writes=["tb1"])
                op("vector", "tensor_tensor", t0[R, :], t0[R, :], t1[R, :], ALU.add, reads=["tb0", "tb1"], writes=["tb0"])
                op("scalar", "activation", sdst, t0[R, 0:512], AF.Sin, scale=ccol[R, 4:5], reads=["tb0", "ccol"], writes=[skey])
                op("scalar", "activation", cdst, t0[R, 512:1024], AF.Sin, scale=ccol[R, 6:7], reads=["tb0", "ccol"], writes=[ckey])
                for j in range(2):
                    for k in range(8):
                        op("tensor", "matmul", psum[0:96, 2 + j, :], wKR[:, k, j * 96:(j + 1) * 96], hT[:, k, :],
                           start=(k == 0), stop=(k == 7), reads=[hkey, ("wKR", k)], writes=[("ps", 2 + j)])
                for k in range(8):
                    op("tensor", "matmul", PS(4), wB[:, k, 256:384], hT[:, k, :], start=(k == 0), stop=(k == 7),
                       reads=[hkey, ("wB", k)], writes=[("ps", 4)])
                if own:
                    for c in range(2):
                        for k in range(8):
                            op("tensor", "matmul", PS(5 + c), wB[:, k, c * 128:(c + 1) * 128], hT[:, k, :], start=(k == 0), stop=(k == 7),
                               reads=[hkey, ("wB", k)], writes=[("ps", 5 + c)])

            def sq_split(c, bank, bcols):
                op("vector", "tensor_scalar", f1[c][:], PS(bank), bcol[:, bcols:bcols + 1], None, ALU.add,
                   reads=[("ps", bank), "bcol"], writes=[("f1", c)])
                op("gpsimd", "tensor_tensor", sq1[c][:], f1[c][:], f1[c][:], ALU.mult, reads=[("f1", c)], writes=[("sq1", c)])
                op("gpsimd", "tensor_copy", sqh[c][:], sq1[c][:], reads=[("sq1", c)], writes=[("sqh", c)])
                op("gpsimd", "tensor_tensor", sql[c][:], sq1[c][:], sqh[c][:], ALU.subtract,
                   reads=[("sq1", c), ("sqh", c)], writes=[("sql", c)])

            def ones_red(n_chunk, norm_n):
                for c in range(n_chunk):
                    op("tensor", "matmul", PS(7), ones_b[:], sqh[c][:], start=(c == 0), stop=False,
                       reads=[("sqh", c), "ones"], writes=[("ps", 7)])
                    op("tensor", "matmul", PS(7), ones_b[:], sql[c][:], start=False, stop=(c == n_chunk - 1),
                       reads=[("sql", c), "ones"], writes=[("ps", 7)])
                op("scalar", "activation", rs1[:], PS(7), AF.Sqrt, bias=epsc[:, 0:1], scale=1.0 / norm_n,
                   reads=[("ps", 7), "epsc"], writes=["rs1"])

            def g2():
                ka, kb_ = kr1[0], kr1[1]
                op("vector", "scalar_tensor_tensor", ka[R, :], psum[R, 2, :], bcol[R, 39:40], cdst, ALU.add, ALU.mult,
                   reads=[("ps", 2), "bcol", ckey], writes=["kr1a"])
                op("vector", "scalar_tensor_tensor", kb_[R, :], psum[R, 3, :], bcol[R, 40:41], sdst, ALU.add, ALU.mult,
                   reads=[("ps", 3), "bcol", skey], writes=["kr1b"])
                op("vector", "tensor_tensor", KT[0][R, g * 512:(g + 1) * 512], ka[R, :], kb_[R, :], ALU.add,
                   reads=["kr1a", "kr1b"], writes=[("KTr0", g)])
                op("gpsimd", "tensor_copy", KT[1][R, g * 512:(g + 1) * 512], KT[0][R, g * 512:(g + 1) * 512],
                   reads=[("KTr0", g)], writes=[("KTr1", g)])
                sq_split(0, 4, 18)

            def g3():
                ones_red(1, 128.0)

            def g4():
                op("vector", "reciprocal", rs1[:], rs1[:], reads=["rs1"], writes=["rs1"])
                op("vector", "scalar_tensor_tensor", ckvT[:, g * 512:(g + 1) * 512], f1[0][:], ccol[:, 2:3], rs1[:],
                   ALU.mult, ALU.mult, reads=[("f1", 0), "rs1", "ccol"], writes=[("ckvT", g)])
                if own:
                    sq_split(0, 5, 16)
                    sq_split(1, 6, 17)

            def g5():
                if own:
                    ones_red(2, 256.0)

            def g6():
                if own:
                    op("vector", "reciprocal", rs1[:], rs1[:], reads=["rs1"], writes=["rs1"])
                    for c in range(2):
                        op("vector", "scalar_tensor_tensor", cqT[:, c, so * 512:(so + 1) * 512], f1[c][:], ccol[:, c:c + 1], rs1[:],
                           ALU.mult, ALU.mult, reads=[("f1", c), "rs1", "ccol"], writes=[("cqT", so, c)])

            return [g1, g2, g3, g4, g5, g6]

        if upto >= 1:
            jobs = [(g, b) for g in range(NG) for b in range(4)]
            due = {}
            SK1 = 3
            for i in range(len(jobs) + 8):
                if i < len(jobs):
                    g, b = jobs[i]
                    if b == 0:
                        op("sync", "dma_start", out=posi[g % 2][64:96, :], in_=pos_d[:, g * 512:(g + 1) * 512],
                           writes=[("posi", g % 2)], dma_slot=f"posi{g % 2}")
                    bi = i % 5
                    r0 = g * 512 + b * 128
                    op("sync", "dma_start", out=xb1[bi][:], in_=x_d[r0:r0 + 128, :], writes=[("xb1", bi)], dma_slot=f"xb1_{bi}")
                    ln_norm(xb1[bi][:], ("xb1", bi), bi)
                if SK1 <= i < len(jobs) + SK1:
                    g, b = jobs[i - SK1]
                    bi = (i - SK1) % 5
                    tr_evac(xb1[bi][:], ("xb1", bi), hT1[g % 2], ("hT1", g % 2), b)
                    if b == 3:
                        for dt_, fn in enumerate(p1_stages(g)):
                            due.setdefault(i + dt_, []).append(fn)
                for fn in due.pop(i, []):
                    fn()
            assert not due


        SC.barrier()
        if upto >= 2:
            for c in range(2):
                load_cast(wuq[:, c, :], w_uq_d[c * 128:(c + 1) * 128, :], 768, ("wuq", c))
                load_cast(wuqs[:, c, :], w_uqs_d[c * 128:(c + 1) * 128, :], 768, ("wuqs", c))
            load_cast(wukv[:], w_ukv_d, 1024, "wukv")
            for vb in range(2):
                op("gpsimd", "memset", Vb[vb][:, :, 64:128], 1.0, writes=[("Vones", vb)])

            SCALE_B = 96.0 ** -0.5
            cv_i = [0]

            def conv_step():
                i = cv_i[0]
                if i >= NCH:
                    return
                cv_i[0] += 1
                kind, c0, nk = CH[i]
                slot = i % 2
                n = nk * 128
                src = WSRC[kind][:, c0:c0 + 128].rearrange("(k p) c -> p k c", p=128)
                op("sync", "dma_start", out=wst[:, slot, 0:n].rearrange("p (k c) -> p k c", c=128), in_=src,
                   writes=[("wst", slot)], dma_slot=f"wst{slot}")
                op("gpsimd", "tensor_copy", cvb[slot][:, 0:n], wst[:, slot, 0:n], reads=[("wst", slot)], writes=[("cvb", slot)])
                op("sync", "dma_start", out=wscr_d[i, :, 0:n], in_=cvb[slot][:, 0:n], reads=[("cvb", slot)],
                   writes=[("wscr", i)], dma_slot=f"cvo{slot}")
            gctr = [0]

            def gen_steps(h):
                buf = h % 2
                steps = []
                for g in range(NG):
                    def kstep(g=g):
                        pb = 6 + (gctr[0] % 2)
                        gctr[0] += 1
                        op("tensor", "matmul", psum[0:64, pb, :], wukv[:, h * 128:h * 128 + 64], ckvT[:, g * 512:(g + 1) * 512],
                           start=True, stop=True, reads=[("ckvT", g), "wukv"], writes=[("ps", pb)])
                        op("vector", "tensor_copy", KT[buf][0:64, g * 512:(g + 1) * 512], psum[0:64, pb, :],
                           reads=[("ps", pb)], writes=[("KT", buf, g)])
                    steps.append(kstep)
                for t8 in range(NT // 8):
                    def vstep(t8=t8):
                        pb = 6 + (gctr[0] % 2)
                        gctr[0] += 1
                        for tt in range(8):
                            t = t8 * 8 + tt
                            op("tensor", "matmul", psum[:, pb, tt * 64:(tt + 1) * 64], ckvT[:, t * 128:(t + 1) * 128],
                               wukv[:, h * 128 + 64:h * 128 + 128], start=True, stop=True,
                               reads=[("ckvT", t // 4), "wukv"], writes=[("ps", pb)])
                        op("vector", "tensor_copy", Vb[buf][:, t8 * 8:(t8 + 1) * 8, 0:64],
                           psum[:, pb, :].rearrange("p (a b) -> p a b", b=64), reads=[("ps", pb)], writes=[("V", buf, t8)])
                    steps.append(vstep)
                return steps

            def gen_q(h, s, qb):
                pq1 = 6 + (gctr[0] % 2)
                pq2 = 6 + ((gctr[0] + 1) % 2)
                gctr[0] += 2
                for c in range(2):
                    op("tensor", "matmul", psum[0:96, pq1, :], wuq[:, c, h * 96:(h + 1) * 96], cqT[:, c, s * 512:(s + 1) * 512],
                       start=(c == 0), stop=(c == 1), reads=[("cqT", s, c), ("wuq", c)], writes=[("ps", pq1)])
                for c in range(2):
                    op("tensor", "matmul", psum[0:96, pq2, :], wuqs[:, c, h * 96:(h + 1) * 96], cqT[:, c, s * 512:(s + 1) * 512],
                       start=(c == 0), stop=(c == 1), reads=[("cqT", s, c), ("wuqs", c)], writes=[("ps", pq2)])
                q = QT[qb]
                op("vector", "tensor_copy", q[0:64, :], psum[0:64, pq1, :], reads=[("ps", pq1)], writes=[("QT", qb)])
                ta, tb_ = qtmp[0], qtmp[1]
                op("vector", "tensor_tensor", ta[R, :], psum[R, pq1, :], cosT[R, s * 512:(s + 1) * 512], ALU.mult,
                   reads=[("ps", pq1), ("cosT", s)], writes=["qta"])
                op("vector", "tensor_tensor", tb_[R, :], psum[R, pq2, :], sinT[R, s * 512:(s + 1) * 512], ALU.mult,
                   reads=[("ps", pq2), ("sinT", s)], writes=["qtb"])
                op("vector", "tensor_tensor", q[R, :], ta[R, :], tb_[R, :], ALU.add, reads=["qta", "qtb"], writes=[("QT", qb)])

            for st_ in gen_steps(0):
                st_()
            qctr = 0
            uctr = 0
            for h in range(8):
                buf = h % 2
                nxt = gen_steps(h + 1) if h < 7 else []
                gen_q(h, 0, qctr % 2)
                for s in range(NP):
                    qb = qctr % 2
                    qctr += 1
                    q = QT[qb]
                    nfull = 8 * s + 4
                    units = []
                    for i in range(nfull // 2):
                        units.append([(2 * i, 0, 512, 0), (2 * i + 1, 0, 512, 512)])
                    t0_ = nfull
                    units.append([(t0_, 0, 512, 0), (t0_ + 1, 128, 384, 512)])
                    units.append([(t0_ + 2, 256, 256, 0), (t0_ + 3, 384, 128, 256)])
                    po = 4 + (s % 2)
                    pend = None
                    first_pv = True

                    for ui, u in enumerate(units):
                        sb_ = uctr % 2
                        pt_i = uctr % 3
                        uctr += 1
                        width = u[-1][3] + u[-1][2]
                        for (t, q0, n, pc) in u:
                            bank = 2 * sb_ + (pc // 512)
                            op("tensor", "matmul", psum[:, bank, (pc % 512):(pc % 512) + n], KT[buf][0:96, t * 128:(t + 1) * 128],
                               q[0:96, q0:q0 + n], start=True, stop=True,
                               reads=[("KT", buf, t // 4), ("KTr%d" % buf, t // 4), ("QT", qb)], writes=[("ps", bank)])
                        sview = psum[:, 2 * sb_:2 * sb_ + 2, :].rearrange("p a b -> p (a b)")[:, 0:width]
                        if u[0][0] < 4:
                            op("scalar", "activation", PT[pt_i][:, 0:width], sview, AF.Exp, bias=FLAG, scale=SCALE_B,
                               reads=[("ps", 2 * sb_), ("ps", 2 * sb_ + 1), "ccol"], writes=[("PT", pt_i)])
                        else:
                            op("scalar", "activation", PT[pt_i][:, 0:width], sview, AF.Exp, scale=SCALE_B,
                               reads=[("ps", 2 * sb_), ("ps", 2 * sb_ + 1)], writes=[("PT", pt_i)])
                        if u[0][0] >= nfull:
                            for (t, q0, n, pc) in u:
                                op("gpsimd", "memset", PT[pt_i][64:128, pc:pc + 64], 0.0, writes=[("PT", pt_i)])
                        if pend is not None:
                            for (t, q0, n, pc) in pend[0]:
                                op("tensor", "matmul", psum[:, po, q0:q0 + n], Vb[buf][:, t, :], PT[pend[1]][:, pc:pc + n],
                                   start=first_pv, stop=False, skip_group_check=True,
                                   reads=[("V", buf, t // 8), ("Vones", buf), ("PT", pend[1])], writes=[("ps", po)])
                                first_pv = False
                        pend = (u, pt_i)
                        if nxt:
                            nxt.pop(0)()
                        if uctr % 4 == 0:
                            conv_step()
                        if ui == 0 and s + 1 < NP:
                            gen_q(h, s + 1, qctr % 2)
                    for (t, q0, n, pc) in pend[0]:
                        op("tensor", "matmul", psum[:, po, q0:q0 + n], Vb[buf][:, t, :], PT[pend[1]][:, pc:pc + n],
                           start=first_pv, stop=False, skip_group_check=True,
                           reads=[("V", buf, t // 8), ("Vones", buf), ("PT", pend[1])], writes=[("ps", po)])
                        first_pv = False
                    rc = rcb[s % 2]
                    c = h // 2
                    lo = (h % 2) * 64
                    op("vector", "reciprocal", rc[0:64, :], psum[64:128, po, :], reads=[("ps", po)], writes=[("rc", s % 2)])
                    op("vector", "tensor_tensor", ybT[lo:lo + 64, c, s * 512:(s + 1) * 512], psum[0:64, po, :], rc[0:64, :], ALU.mult,
                       reads=[("ps", po), ("rc", s % 2)], writes=[("ybT", c, s)])
                while nxt:
                    nxt.pop(0)()
            while cv_i[0] < NCH:
                conv_step()

        stores = []
        if dbg and upto >= 2:
            stores.append(op("sync", "dma_start", out=dbg_yb_d, in_=ybT[:],
                             reads=[("ybT", c, s) for c in range(4) for s in range(NP)], dma_slot="dbg0"))
        SC.barrier()

        a3 = Bump(0)
        BT = a3([128, 8, 640], F32)
        lnbc = a3([128, 4, D], F32)
        xo = a3([128, 4, D], F32)
        xh = [a3([128, D], F32) for _ in range(3)]
        hTe = a3([128, 8, 512], BF16)
        hTw = a3([128, 8, 512], BF16)
        wch = [a3([128, 8, 128], BF16) for _ in range(7)]
        wbufs = list(wch) + [wst[:, sl, hf * 512:(hf + 1) * 512].bitcast(BF16).rearrange("p (k c) -> p k c", c=128)
                             for sl in range(2) for hf in range(2)]
        aqT = [a3([128, 512], BF16) for _ in range(2)]
        akT = [a3([128, 1024], BF16) for _ in range(2)]
        Va = [a3([128, 8, 2, 128], BF16) for _ in range(2)]
        silu2 = [a3([128, 512], F32) for _ in range(2)]
        silu2b = [a3([128, 512], F32) for _ in range(2)]
        zbt = [a3([128, 512], F32) for _ in range(2)]
        tzt = [a3([128, 512], F32) for _ in range(2)]
        Stmp = [a3([128, 512], F32) for _ in range(3)]
        PTa = [a3([128, 512], BF16) for _ in range(4)]
        rca = [a3([128, 512], F32) for _ in range(2)]
        att = [a3([128, 512], F32) for _ in range(2)]
        zaT = a3([128, 4, 512], BF16)
        zbT = a3([128, 4, 512], BF16)
        tat = [a3([128, 512], F32) for _ in range(2)]
        tbt = [a3([128, 512], F32) for _ in range(2)]
        t1t = [a3([128, 512], F32) for _ in range(2)]
        t2t = [a3([128, 512], F32) for _ in range(2)]
        mixT = a3([128, 8, 512], BF16)
        amk = a3([128, 640], F32)

        if upto >= 3:
            op("sync", "dma_start", out=BT[:], in_=bt_d, writes=["BT"], dma_slot="c4")
            op("sync", "dma_start", out=amk[:], in_=amask_d, writes=["amk"], dma_slot="c5")
            op("sync", "dma_start", out=lnbc[:], in_=lnbc_d, writes=["lnbc"], dma_slot="c6")
            for h in range(8):
                op("gpsimd", "tensor_tensor", BT[:, h, :], BT[:, h, :], amk[:], ALU.add, reads=["BT", "amk"], writes=["BT"])
            op("gpsimd", "tensor_scalar", lnbc[:, 0:2, :], lnbc[:, 0:2, :], ALPHA, None, ALU.mult, reads=["lnbc"], writes=["lnbc"])
            for vb in range(2):
                op("gpsimd", "memset", Va[vb][:, :, :, 64:128], 1.0, writes=[("VaOnes", vb)])

            wctr = [0]
            cast_ctr = [0]

            wissued = [0]
            TOTCH = NCH * NP

            def stream_chunk(src_d, c0, nk):
                i = wctr[0]
                wctr[0] += 1
                ci = i % NCH
                assert CH[ci][1] == c0 and CH[ci][2] == nk, (ci, CH[ci], c0, nk)
                while wissued[0] < min(TOTCH, i + len(wbufs)):
                    ii = wissued[0]
                    wissued[0] += 1
                    cj = ii % NCH
                    nkk = CH[cj][2]
                    jb = ii % len(wbufs)
                    op("sync", "dma_start", out=wbufs[jb][:, 0:nkk, :],
                       in_=wscr_d[cj, :, 0:nkk * 128].rearrange("p (k c) -> p k c", c=128),
                       reads=[("wscr", cj)], writes=[("wch", jb)], dma_slot=f"wch{jb}")
                j = i % len(wbufs)
                return wbufs[j], ("wch", j)

            ipb = [0]

            def inproj(wt, wkey, hT, hkey):
                bank = ipb[0] % 2
                ipb[0] += 1
                for k in range(8):
                    op("tensor", "matmul", PS(bank), wt[:, k, :], hT[:, k, :], start=(k == 0), stop=(k == 7),
                       reads=[wkey, hkey], writes=[("ps", bank)])
                return bank

            SCALE_A = 0.125
            JORDER = [3, 4, 2, 5, 1, 6, 0, 7]
            JN = {0: (0, 128), 1: (0, 256), 2: (0, 384), 3: (0, 512), 4: (0, 512), 5: (128, 384), 6: (256, 256), 7: (384, 128)}
            sctr = [0]
            evc = [0]

            def prep_jobs(s):
                e_g, w_g = 2 * s, 2 * s + 1
                return [(g_, hT_, hk_, b) for (g_, hT_, hk_) in ((e_g, hTe, "hTe"), (w_g, hTw, "hTw")) for b in range(4)]

            def prepA(job, idx):
                g_, hT_, hk_, b = job
                xi = idx % 3
                r0 = g_ * 512 + b * 128
                op("sync", "dma_start", out=xh[xi][:], in_=x_d[r0:r0 + 128, :], writes=[("xh", xi)], dma_slot=f"xh{xi}")
                ln_norm(xh[xi][:], ("xh", xi), xi)

            def prepB(job, idx):
                g_, hT_, hk_, b = job
                xi = idx % 3
                tr_evac(xh[xi][:], ("xh", xi), hT_, hk_, b, banks=(2, 3))

            def post_ln(s):
                cols, k4 = ln_stats4(xo, [("xo", b) for b in range(4)])
                for b in range(4):
                    xt = xo[:, b, :]
                    rstd, nmr = cols[b]
                    op("scalar", "activation", xt, xt, AF.Identity, bias=nmr, scale=rstd, reads=[("xo", b), k4], writes=[("xo", b)])
                    op("gpsimd", "tensor_tensor", xt, xt, lnbc[:, 2, :], ALU.mult, reads=[("xo", b), "lnbc"], writes=[("xo", b)])
                    op("gpsimd", "tensor_tensor", xt, xt, lnbc[:, 3, :], ALU.add, reads=[("xo", b), "lnbc"], writes=[("xo", b)])
                    r0 = s * 512 + b * 128
                    stores.append(op("sync", "dma_start", out=out_d[r0:r0 + 128, :], in_=xt, reads=[("xo", b)], dma_slot=f"st{b}"))

            def resid_load(s):
                w_g = 2 * s + 1
                op("sync", "dma_start", out=xo[:], in_=x_d[w_g * 512:(w_g + 1) * 512, :].rearrange("(b p) d -> p b d", p=128),
                   writes=[("xo", b) for b in range(4)], dma_slot="xo")

            def resid_prep(s):
                cols, k4 = ln_stats4(xo, [("xo", b) for b in range(4)])
                for b in range(4):
                    xt = xo[:, b, :]
                    rstd, nmr = cols[b]
                    op("scalar", "activation", xt, xt, AF.Identity, bias=nmr, scale=rstd, reads=[("xo", b), k4], writes=[("xo", b)])
                    op("gpsimd", "tensor_tensor", xt, xt, lnbc[:, 0, :], ALU.mult, reads=[("xo", b), "lnbc"], writes=[("xo", b)])
                    op("gpsimd", "tensor_tensor", xt, xt, lnbc[:, 1, :], ALU.add, reads=[("xo", b), "lnbc"], writes=[("xo", b)])

            xhc = [0]
            fin_defer = []
            hctr = [0]
            if p3cut >= 1:
                j0 = prep_jobs(0)
                for i in range(10):
                    if i < 8:
                        prepA(j0[i], i)
                    if i >= 2:
                        prepB(j0[i - 2], i - 2)
            for s in range(NP):
                if p3cut < 2:
                    continue
                def inproj_jobs(c):
                    cb = c % 2
                    hold = {}

                    def j_ak(hi):
                        def f():
                            if hi == 0:
                                hold["ak"] = stream_chunk(w_in_d, C_AK + c * 128, 8)
                            wt, wkey = hold["ak"]
                            hT_, hk_ = ((hTe, "hTe"), (hTw, "hTw"))[hi]
                            bank = inproj(wt, wkey, hT_, hk_)
                            op("scalar", "activation", akT[cb][:, hi * 512:(hi + 1) * 512], PS(bank), AF.Identity, bias=bcol[:, 4 + c:5 + c],
                               reads=[("ps", bank), "bcol"], writes=[("akT", cb)])
                        return f

                    def j_aq():
                        wt, wkey = stream_chunk(w_in_d, C_AQ + c * 128, 8)
                        bank = inproj(wt, wkey, hTw, "hTw")
                        op("scalar", "activation", aqT[cb][:], PS(bank), AF.Identity, bias=bcol[:, 0 + c:1 + c],
                           reads=[("ps", bank), "bcol"], writes=[("aqT", cb)])

                    def j_av(half, b):
                        def f():
                            if half == 0 and b == 0:
                                hold["av"] = stream_chunk(w_in_d, C_AV + c * 128, 8)
                            wt, wkey = hold["av"]
                            hT_, hk_ = ((hTe, "hTe"), (hTw, "hTw"))[half]
                            bank = 2 + half
                            for k in range(8):
                                op("tensor", "matmul", psum[:, bank, b * 128:(b + 1) * 128], hT_[:, k, b * 128:(b + 1) * 128], wt[:, k, :],
                                   start=(k == 0), stop=(k == 7), reads=[wkey, hk_], writes=[("ps", bank)])
                            if b == 3:
                                op("scalar", "activation", Va[cb][:, half * 4:(half + 1) * 4, :, 0:64],
                                   psum[:, bank, :].rearrange("p (b h d) -> p b h d", h=2, d=64), AF.Copy,
                                   reads=[("ps", bank)], writes=[("Va", cb)])
                        return f

                    def j_az():
                        wt, wkey = stream_chunk(w_in_d, C_AZ + c * 128, 8)
                        bank = inproj(wt, wkey, hTw, "hTw")
                        op("scalar", "activation", zbt[cb][:], PS(bank), AF.Identity, bias=bcol[:, 12 + c:13 + c],
                           reads=[("ps", bank), "bcol"], writes=[("zbt", cb)])
                        op("scalar", "activation", tzt[cb][:], zbt[cb][:], AF.Tanh, scale=0.5,
                           reads=[("zbt", cb)], writes=[("tzt", cb)])
                        op("vector", "scalar_tensor_tensor", silu2[cb][:], tzt[cb][:], 1.0, zbt[cb][:], ALU.add, ALU.mult,
                           reads=[("tzt", cb), ("zbt", cb)], writes=[("silu2", cb)])
                    return ([j_ak(0), j_ak(1), j_aq] + [j_av(hf, b) for hf in range(2) for b in range(4)] + [j_az])

                def bgate_jobs():
                    jobs = []
                    for c in range(4):
                        def j_bz(c=c):
                            cb = c % 2
                            wt, wkey = stream_chunk(w_in_d, C_BZ + c * 128, 8)
                            bank = inproj(wt, wkey, hTw, "hTw")
                            op("scalar", "activation", zbt[cb][:], PS(bank), AF.Identity, bias=bcol[:, 19 + c:20 + c],
                               reads=[("ps", bank), "bcol"], writes=[("zbt", cb)])
                            op("scalar", "activation", tzt[cb][:], zbt[cb][:], AF.Tanh, scale=0.5,
                               reads=[("zbt", cb)], writes=[("tzt", cb)])
                            op("vector", "scalar_tensor_tensor", silu2b[cb][:], tzt[cb][:], 1.0, zbt[cb][:], ALU.add, ALU.mult,
                               reads=[("tzt", cb), ("zbt", cb)], writes=[("silu2b", cb)])
                            op("gpsimd", "tensor_tensor", zbT[:, c, :], ybT[:, c, s * 512:(s + 1) * 512], silu2b[cb][:], ALU.mult,
                               reads=[("ybT", c, s), ("silu2b", cb)], writes=[("zbT", c)])
                        jobs.append(j_bz)
                    return jobs

                def attention(c, side_jobs):
                    cb = c % 2
                    steps = []
                    for hh in range(2):
                        po = 6 + (hctr[0] % 2)
                        hctr[0] += 1
                        for ji, j in enumerate(JORDER):
                            steps.append((hh, po, j, ji == 0, ji == 7))

                    def qk_stage(st):
                        hh, po, j, first, last = st
                        h = 2 * c + hh
                        lo = hh * 64
                        q0, n = JN[j]
                        u0 = q0 - (j - 4) * 128
                        sbk = 4 + (sctr[0] % 2)
                        si = sctr[0] % 3
                        pi_ = sctr[0] % 4
                        sctr[0] += 1
                        op("tensor", "matmul", psum[:, sbk, 0:n], akT[cb][lo:lo + 64, j * 128:(j + 1) * 128], aqT[cb][lo:lo + 64, q0:q0 + n],
                           start=True, stop=True, reads=[("akT", cb), ("aqT", cb)], writes=[("ps", sbk)])
                        op("vector", "scalar_tensor_tensor", Stmp[si][:, 0:n], psum[:, sbk, 0:n], SCALE_A, BT[:, h, u0:u0 + n], ALU.mult, ALU.add,
                           reads=[("ps", sbk), "BT"], writes=[("Stmp", si)])
                        if s == 0 and j < 4:
                            op("scalar", "activation", PTa[pi_][:, 0:n], Stmp[si][:, 0:n], AF.Exp, bias=FLAG,
                               reads=[("Stmp", si), "ccol"], writes=[("PTa", pi_)])
                        else:
                            op("scalar", "activation", PTa[pi_][:, 0:n], Stmp[si][:, 0:n], AF.Exp,
                               reads=[("Stmp", si)], writes=[("PTa", pi_)])
                        if fin_defer:
                            fin_defer.pop(0)()
                        return pi_

                    def pv_stage(st, pi_):
                        hh, po, j, first, last = st
                        h = 2 * c + hh
                        lo = hh * 64
                        q0, n = JN[j]
                        op("tensor", "matmul", psum[:, po, q0:q0 + n], Va[cb][:, j, hh, :], PTa[pi_][:, 0:n],
                           start=first, stop=False, skip_group_check=True,
                           reads=[("Va", cb), ("VaOnes", cb), ("PTa", pi_)], writes=[("ps", po)])
                        if last:
                            ri = h % 2
                            for pc_ in range(4):
                                def fin(pc_=pc_, ri=ri, lo=lo, po=po, c=c, cb=cb):
                                    cs_ = slice(pc_ * 128, (pc_ + 1) * 128)
                                    op("vector", "reciprocal", rca[ri][0:64, cs_], psum[64:128, po, cs_], reads=[("ps", po)], writes=[("rca", ri)])
                                    op("vector", "tensor_tensor", att[ri][lo:lo + 64, cs_], psum[0:64, po, cs_], rca[ri][0:64, cs_], ALU.mult,
                                       reads=[("ps", po), ("rca", ri)], writes=[("att", ri)])
                                    op("vector", "scalar_tensor_tensor", zaT[lo:lo + 64, c, cs_], att[ri][lo:lo + 64, cs_], bcol[lo:lo + 64, 8 + c:9 + c],
                                       silu2[cb][lo:lo + 64, cs_], ALU.add, ALU.mult,
                                       reads=[("att", ri), "bcol", ("silu2", cb)], writes=[("zaT", c)])
                                fin_defer.append(fin)

                    LOOK = 2
                    pis = {}
                    side = list(side_jobs)
                    every = 1 if len(side) > 8 else max(1, (len(steps) - 2) // max(1, len(side)))
                    for i in range(len(steps) + LOOK):
                        if i < len(steps):
                            pis[i] = qk_stage(steps[i])
                        if i - LOOK >= 0:
                            pv_stage(steps[i - LOOK], pis[i - LOOK])
                        if side and i >= 1 and (i - 1) % every == 0:
                            side.pop(0)()
                    while side:
                        side.pop(0)()

                for jb in inproj_jobs(0):
                    jb()
                if s > 0 and p3cut >= 6:
                    post_ln(s - 1)
                resid_load(s)
                for c in range(4):
                    attention(c, inproj_jobs(c + 1) if c < 3 else (bgate_jobs() if p3cut >= 3 else []))
                while fin_defer:
                    fin_defer.pop(0)()
                if p3cut < 3:
                    continue
                resid_prep(s)

                if p3cut < 4:
                    continue
                for oc in range(8):
                    ob = oc % 2
                    wt, wkey = stream_chunk(w_in_d, C_GA + oc * 128, 8)
                    bank = inproj(wt, wkey, hTw, "hTw")
                    op("scalar", "activation", tat[ob][:], PS(bank), AF.Tanh, bias=bcolh[:, 23 + oc:24 + oc], scale=0.5,
                       reads=[("ps", bank), "bcolh"], writes=[("tat", ob)])
                    wt, wkey = stream_chunk(w_pa_d, oc * 128, 4)
                    for c in range(4):
                        op("tensor", "matmul", PS(7), wt[:, c, :], zaT[:, c, :], start=(c == 0), stop=(c == 3),
                           reads=[wkey, ("zaT", c)], writes=[("ps", 7)])
                    op("vector", "scalar_tensor_tensor", t1t[ob][:], tat[ob][:], 1.0, PS(7), ALU.add, ALU.mult,
                       reads=[("tat", ob), ("ps", 7)], writes=[("t1t", ob)])
                    wt, wkey = stream_chunk(w_in_d, C_GB + oc * 128, 8)
                    bank = inproj(wt, wkey, hTw, "hTw")
                    op("scalar", "activation", tbt[ob][:], PS(bank), AF.Tanh, bias=bcolh[:, 31 + oc:32 + oc], scale=0.5,
                       reads=[("ps", bank), "bcolh"], writes=[("tbt", ob)])
                    wt, wkey = stream_chunk(w_pb_d, oc * 128, 4)
                    for c in range(4):
                        op("tensor", "matmul", PS(6), wt[:, c, :], zbT[:, c, :], start=(c == 0), stop=(c == 3),
                           reads=[wkey, ("zbT", c)], writes=[("ps", 6)])
                    op("vector", "scalar_tensor_tensor", t2t[ob][:], tbt[ob][:], 1.0, PS(6), ALU.add, ALU.mult,
                       reads=[("tbt", ob), ("ps", 6)], writes=[("t2t", ob)])
                    op("gpsimd", "tensor_tensor", mixT[:, oc, :], t1t[ob][:], t2t[ob][:], ALU.add,
                       reads=[("t1t", ob), ("t2t", ob)], writes=[("mixT", oc)])
                if p3cut < 5:
                    continue
                pj = prep_jobs(s + 1) if s + 1 < NP else []
                if pj:
                    prepA(pj[0], 0)
                    prepA(pj[1], 1)
                for cc in range(8):
                    if pj and cc + 2 < 8:
                        prepA(pj[cc + 2], cc + 2)
                    if pj:
                        prepB(pj[cc], cc)
                    wt, wkey = stream_chunk(w_out_d, cc * 128, 8)
                    bank = 4 + (cc % 2)
                    for b in range(4):
                        for oc in range(8):
                            op("tensor", "matmul", psum[:, bank, b * 128:(b + 1) * 128], mixT[:, oc, b * 128:(b + 1) * 128], wt[:, oc, :],
                               start=(oc == 0), stop=(oc == 7), reads=[wkey, ("mixT", oc)], writes=[("ps", bank)])
                    op("vector", "scalar_tensor_tensor", xo[:, :, cc * 128:(cc + 1) * 128],
                       psum[:, bank, :].rearrange("p (b d) -> p b d", d=128), 0.25, xo[:, :, cc * 128:(cc + 1) * 128], ALU.mult, ALU.add,
                       reads=[("ps", bank)] + [("xo", b) for b in range(4)], writes=[("xo", b) for b in range(4)])
                if p3cut < 6:
                    continue
                if s == NP - 1:
                    post_ln(s)


        SC.emit(nc, es, final_waits=stores)
    return nc


def host_consts():
    c = {}
    c["ident"] = np.eye(128, dtype=np.float32)
    k = np.arange(128)[:, None]
    u = np.arange(640)[None, :]
    kc = k // 64
    uc = u // 64
    vis = (kc <= uc) & (kc >= uc - 8)
    c["amask"] = np.where(vis, 0.0, NEG).astype(np.float32)
    c["bt_idx"] = (np.clip(u - k, -128, 128) + 128).astype(np.int64)
    return c


def core_inputs(inp, b, p, S=SEQ, consts=None):
    cs = consts or host_consts()
    f32 = np.float32
    x = np.asarray(inp["x"][b], dtype=f32)
    pos = np.asarray(inp["positions"][b], dtype=np.int32)
    if p == 0:
        xl = np.concatenate([np.zeros((512, D), f32), x[:S - 512]], axis=0)
        pl = np.concatenate([np.zeros((512,), np.int32), pos[:S - 512]], axis=0)
    else:
        xl, pl = x[:S], pos[:S]
    b_in = np.asarray(inp["b_in"][0], f32)
    w_in = np.ascontiguousarray(np.asarray(inp["w_in"][0], f32))
    starts = ([C_AQ + 128 * i for i in range(4)] + [C_AK + 128 * i for i in range(4)] + [C_AV + 128 * i for i in range(4)]
              + [C_AZ + 128 * i for i in range(4)] + [C_CQ, C_CQ + 128, C_CKV] + [C_BZ + 128 * i for i in range(4)]
              + [C_GA + 128 * i for i in range(8)] + [C_GB + 128 * i for i in range(8)])
    bcol = np.zeros((128, 44), f32)
    for i, st in enumerate(starts):
        bcol[:, i] = b_in[st:st + 128]
    bkr = b_in[C_KR:C_KR + 32]
    bcol[64:96, 39] = bkr
    bcol[64:80, 40] = bkr[16:32]
    bcol[80:96, 40] = bkr[0:16]
    ccol = np.zeros((128, 8), f32)
    qg = np.asarray(inp["q_norm_g"][0], f32)
    ccol[:, 0] = qg[0:128]
    ccol[:, 1] = qg[128:256]
    ccol[:, 2] = np.asarray(inp["kv_norm_g"][0], f32)
    inv_freq = (10000.0 ** (-np.arange(16, dtype=np.float32) / 16)).astype(f32)
    r = np.arange(32)
    ccol[64:96, 3] = (inv_freq[r % 16].astype(np.float64) / TWO_PI).astype(f32)
    tp = np.float32(6.28318)
    ccol[64:80, 4] = -tp
    ccol[80:96, 4] = tp
    ccol[64:96, 6] = tp
    ccol[:, 5] = NEG if p == 0 else 0.0
    lncol = np.zeros((128, 16), f32)
    lncol[:, 0:8] = np.asarray(inp["ln_in_g"], f32).reshape(8, 128).T
    lncol[:, 8:16] = np.asarray(inp["ln_in_b"], f32).reshape(8, 128).T
    lnbc = np.stack([np.broadcast_to(np.asarray(v, f32).reshape(1, D), (128, D)) for v in
                     (inp["ln_in_g"], inp["ln_in_b"], inp["ln_post_g"][0], inp["ln_post_b"][0])], axis=1)
    rb = np.asarray(inp["rel_bias"][0], f32)
    bt = np.ascontiguousarray(np.transpose(rb[cs["bt_idx"]], (0, 2, 1)))
    w_krp = np.zeros((D, 192), f32)
    w_krp[:, 64:96] = w_in[:, C_KR:C_KR + 32]
    w_krp[:, 160:176] = w_in[:, C_KR + 16:C_KR + 32]
    w_krp[:, 176:192] = w_in[:, C_KR:C_KR + 16]
    w_uq = np.ascontiguousarray(np.asarray(inp["w_uq"][0], f32))
    w_uqs = np.zeros_like(w_uq)
    for h in range(8):
        w_uqs[:, h * 96 + 64:h * 96 + 80] = w_uq[:, h * 96 + 80:h * 96 + 96]
        w_uqs[:, h * 96 + 80:h * 96 + 96] = w_uq[:, h * 96 + 64:h * 96 + 80]
    return {
        "x": np.ascontiguousarray(xl), "pos32": np.ascontiguousarray(np.broadcast_to(pl[None, :], (32, S))),
        "ident": cs["ident"], "bcol": bcol, "ccol": ccol, "lncol": lncol, "lnbc": np.ascontiguousarray(lnbc),
        "bt": bt, "amask": cs["amask"], "w_in": w_in, "w_krp": w_krp, "w_uq": w_uq, "w_uq_sw": w_uqs,
        "w_ukv": np.ascontiguousarray(np.asarray(inp["w_ukv"][0], f32)),
        "w_proj_a": np.ascontiguousarray(np.asarray(inp["w_proj_a"][0], f32)),
        "w_proj_b": np.ascontiguousarray(np.asarray(inp["w_proj_b"][0], f32)),
        "w_out": np.ascontiguousarray(np.asarray(inp["w_out"][0], f32)),
    }


def assemble(results, n_batch, S=SEQ):
    out = np.zeros((n_batch, S, D), np.float32)
    for b in range(n_batch):
        for p in range(2):
            o = np.asarray(results[b * 2 + p]["out"], np.float32)
            for s in range(S // 1024):
                gg = 2 * s + p
                out[b, gg * 512:(gg + 1) * 512, :] = o[s * 512:(s + 1) * 512, :]
    return out


_NC_CACHE = {}


def kernel(**inputs):
    inp = {k: np.asarray(v) for k, v in inputs.items()}
    nb = inp["x"].shape[0]
    S = inp["x"].shape[1]
    if S not in _NC_CACHE:
        _NC_CACHE[S] = build_nc(S)
    nc = _NC_CACHE[S]
    cs = host_consts()
    in_maps = [core_inputs(inp, b, p, S, cs) for b in range(nb) for p in range(2)]
    res = run_bass_kernel_spmd(nc, in_maps, core_ids=list(range(2 * nb)))
    return assemble(res.results, nb, S)
```

```python
import math
from contextlib import ExitStack

import numpy as np
import concourse.bass as bass
import concourse.mybir as mybir
from concourse.bass_utils import run_bass_kernel_spmd

F32 = mybir.dt.float32
BF16 = mybir.dt.bfloat16
I32 = mybir.dt.int32
AF = mybir.ActivationFunctionType
ALU = mybir.AluOpType

D = 1024
SEQ = 8192
BATCH = 4
IN_COLS = 5024
C_AQ, C_AK, C_AV, C_AZ, C_CQ, C_CKV, C_KR, C_BZ, C_GA, C_GB = 0, 512, 1024, 1536, 2048, 2304, 2432, 2464, 2976, 4000
ALPHA = 2.0 ** 0.25
LN_EPS = 1e-5
RMS_EPS = 1e-6
NEG = -30000.0
TWO_PI = 2.0 * math.pi

ENGS = ("tensor", "vector", "scalar", "gpsimd", "sync")
SEM_EPOCH = 30000


class Op:
    __slots__ = ("eng", "name", "args", "kw", "deps", "signal", "idx", "dma", "dsem", "dval")

    def __init__(self, eng, name, args, kw, dma):
        self.eng = eng
        self.name = name
        self.args = args
        self.kw = kw
        self.deps = []
        self.signal = False
        self.idx = -1
        self.dma = dma
        self.dsem = None
        self.dval = 0


class Sched:
    def __init__(self):
        self.q = {e: [] for e in ENGS}
        self.last_w = {}
        self.readers = {}
        self.dma_slots = {}
        self.slot_names = []
        self.all_dma = []
        self.pending = {e: [] for e in ENGS}
        self.ps_read = {}

    def op(self, eng, name, *args, reads=(), writes=(), dma_slot=None, **kw):
        o = Op(eng, name, args, kw, dma_slot is not None)
        o.idx = len(self.q[eng])
        deps = list(self.pending[eng])
        self.pending[eng] = []
        ex = [r for r in reads if isinstance(r, tuple) and r[0] == "ps"]
        if ex:
            reads = [r for r in reads if not (isinstance(r, tuple) and r[0] == "ps")]
            for r in ex:
                w = self.last_w.get(r)
                if w is not None and not (w.eng == eng and self.ps_read.get(r, False) and not w.dma):
                    deps.append(w)
                deps.extend(self.readers.get(r, ()))
                self.last_w[r] = o
                self.readers[r] = []
                self.ps_read[r] = True
        for r in writes:
            if isinstance(r, tuple) and r[0] == "ps":
                self.ps_read[r] = False
        for r in reads:
            w = self.last_w.get(r)
            if w is not None:
                deps.append(w)
        for r in writes:
            w = self.last_w.get(r)
            if w is not None:
                deps.append(w)
            deps.extend(self.readers.get(r, ()))
        for r in reads:
            lst = self.readers.setdefault(r, [])
            if not o.dma:
                lst[:] = [x for x in lst if x.dma or x.eng != eng]
            lst.append(o)
        for r in writes:
            self.last_w[r] = o
            self.readers[r] = []
        seen = set()
        for d in deps:
            if d is o or id(d) in seen:
                continue
            seen.add(id(d))
            if (not d.dma) and d.eng == eng and eng == "tensor":
                continue
            o.deps.append(d)
            d.signal = True
        if dma_slot is not None:
            if dma_slot not in self.dma_slots:
                self.dma_slots[dma_slot] = 0
                self.slot_names.append(dma_slot)
            self.dma_slots[dma_slot] += 16
            o.dsem = dma_slot
            o.dval = self.dma_slots[dma_slot]
            self.all_dma.append(o)
        self.q[eng].append(o)
        return o

    def barrier(self):
        lasts = []
        for e in ENGS:
            comp = [o for o in self.q[e] if not o.dma]
            if comp:
                lasts.append(comp[-1])
        last_dma = {}
        for o in self.all_dma:
            last_dma[o.dsem] = o
        lasts.extend(last_dma.values())
        for e in ENGS:
            self.pending[e] = list(self.pending[e]) + lasts

    def emit(self, nc, es, final_waits=()):
        n_sig = {e: sum(1 for o in self.q[e] if o.signal and not o.dma) for e in ENGS}
        esems = {}
        for e in ENGS:
            n_ep = n_sig[e] // SEM_EPOCH + 1
            esems[e] = [es.enter_context(nc.semaphore(f"s_{e}_{k}")) for k in range(n_ep)]
        dsems = {}
        for i, s in enumerate(self.slot_names):
            dsems[s] = es.enter_context(nc.semaphore(f"d_{i}"))
        for e in ENGS:
            c = 0
            for o in self.q[e]:
                if (not o.dma) and o.signal:
                    c += 1
                    o.dval = c

        def ev_of(d):
            if d.dma:
                return dsems[d.dsem], d.dval, None, 0
            ep = (d.dval - 1) // SEM_EPOCH
            return esems[d.eng][ep], d.dval - ep * SEM_EPOCH, d.eng, ep

        block = es.enter_context(nc.Block())
        sched = self

        def make(e):
            def body(eng):
                waited = {}
                max_ep = {}
                for o in sched.q[e]:
                    for d in o.deps:
                        sem, val, deng, ep = ev_of(d)
                        if deng is not None and max_ep.get(deng, -1) > ep:
                            continue
                        if waited.get(sem.name, 0) >= val:
                            continue
                        waited[sem.name] = val
                        if deng is not None:
                            max_ep[deng] = max(max_ep.get(deng, -1), ep)
                        eng.wait_ge(sem, val)
                    ins = getattr(eng, o.name)(*o.args, **o.kw)
                    if o.dma:
                        ins.then_inc(dsems[o.dsem], 16)
                    elif o.signal:
                        ep = (o.dval - 1) // SEM_EPOCH
                        ins.then_inc(esems[e][ep], 1)
                if e == "sync":
                    for d in final_waits:
                        sem, val, _, _ = ev_of(d)
                        eng.wait_ge(sem, val)
            return body

        block.tensor(make("tensor"))
        block.vector(make("vector"))
        block.scalar(make("scalar"))
        block.gpsimd(make("gpsimd"))
        block.sync(make("sync"))


def build_nc(S=SEQ, dbg=False, upto=3, p1cut=9, p3cut=9):
    assert S % 1024 == 0
    NG = S // 512
    NP = NG // 2
    SO = S // 2
    NT = S // 128

    nc = bass.Bass("TRN2", target_bir_lowering=False, dynamic_dma_scratch_size=1024)

    def dram(n, sh, dt, kind="ExternalInput"):
        return nc.dram_tensor(n, sh, dt, kind=kind).ap()

    x_d = dram("x", [S, D], F32)
    pos_d = dram("pos32", [32, S], I32)
    ident_d = dram("ident", [128, 128], F32)
    bcol_d = dram("bcol", [128, 44], F32)
    ccol_d = dram("ccol", [128, 8], F32)
    lncol_d = dram("lncol", [128, 16], F32)
    lnbc_d = dram("lnbc", [128, 4, D], F32)
    bt_d = dram("bt", [128, 8, 640], F32)
    amask_d = dram("amask", [128, 640], F32)
    w_in_d = dram("w_in", [D, IN_COLS], F32)
    w_krp_d = dram("w_krp", [D, 192], F32)
    w_uq_d = dram("w_uq", [256, 768], F32)
    w_uqs_d = dram("w_uq_sw", [256, 768], F32)
    w_ukv_d = dram("w_ukv", [128, 1024], F32)
    w_pa_d = dram("w_proj_a", [512, D], F32)
    w_pb_d = dram("w_proj_b", [512, D], F32)
    w_out_d = dram("w_out", [D, D], F32)
    out_d = dram("out", [SO, D], F32, kind="ExternalOutput")
    CH = []
    for c in range(4):
        CH += [("in", C_AK + c * 128, 8), ("in", C_AQ + c * 128, 8), ("in", C_AV + c * 128, 8), ("in", C_AZ + c * 128, 8)]
    for c in range(4):
        CH.append(("in", C_BZ + c * 128, 8))
    for oc in range(8):
        CH += [("in", C_GA + oc * 128, 8), ("pa", oc * 128, 4), ("in", C_GB + oc * 128, 8), ("pb", oc * 128, 4)]
    for cc in range(8):
        CH.append(("out", cc * 128, 8))
    NCH = len(CH)
    wscr_d = dram("wscr", [NCH, 128, 1024], BF16, kind="Internal")
    WSRC = {"in": w_in_d, "pa": w_pa_d, "pb": w_pb_d, "out": w_out_d}
    if dbg:
        dbg_yb_d = dram("dbg_yb", [128, 4, SO], BF16, kind="ExternalOutput")

    SC = Sched()
    op = SC.op
    es = ExitStack()
    with es:
        def sbt(n, sh, dt):
            return es.enter_context(nc.sbuf_tensor(n, sh, dt))

        ident = sbt("ident_s", [128, 128], F32)
        ones_b = sbt("ones_b", [128, 128], BF16)
        bcol = sbt("bcol_s", [128, 44], F32)
        bcolh = sbt("bcolh_s", [128, 44], F32)
        ccol = sbt("ccol_s", [128, 8], F32)
        lncol = sbt("lncol_s", [128, 16], F32)
        mhalf = sbt("mhalf_s", [128, 8], F32)
        epsc = sbt("epsc_s", [128, 2], F32)
        stat = sbt("stat_s", [128, 96], F32)
        stat4 = sbt("stat4_s", [128, 64], F32)
        wst = sbt("wst_s", [128, 2, 1024], F32)
        ybT = sbt("ybT_s", [128, 4, SO], BF16)
        psum = es.enter_context(nc.psum_tensor("ps", [128, 8, 512], F32))

        P3 = 180 * 1024
        P12 = 12 * S + 72 * 1024
        ARENA = max(P12, P3)
        arena = sbt("arena", [128, ARENA // 2], BF16)

        def view(off, shape, dt):
            n = 1
            for s_ in shape[1:]:
                n *= s_
            esz = 4 if dt in (F32, I32) else 2
            assert off % 4 == 0 and off + n * esz <= ARENA, (off, shape, ARENA)
            v = arena[:, off // 2: off // 2 + n * esz // 2]
            if esz == 4:
                v = v.bitcast(dt)
            if len(shape) == 3:
                v = v.rearrange("p (a b) -> p a b", b=shape[2])
            elif len(shape) == 4:
                v = v.rearrange("p (a b c) -> p a b c", b=shape[2], c=shape[3])
            return v

        class Bump:
            def __init__(self, off=0):
                self.off = off

            def __call__(self, shape, dt):
                n = 1
                for s_ in shape[1:]:
                    n *= s_
                esz = 4 if dt in (F32, I32) else 2
                v = view(self.off, shape, dt)
                self.off += (n * esz + 31) // 32 * 32
                return v

        al = Bump(0)
        ckvT = al([128, S], BF16)
        cqT = al([128, 2, SO], BF16)
        KT = [al([128, S], BF16), al([128, S], BF16)]
        cosT = al([128, SO], F32)
        sinT = al([128, SO], F32)
        p2_base = al.off
        Vb = [al([128, NT, 128], BF16), al([128, NT, 128], BF16)]
        QT = [al([128, 512], BF16), al([128, 512], BF16)]
        PT = [al([128, 1024], BF16) for _ in range(3)]
        rcb = [al([128, 512], F32) for _ in range(2)]
        qtmp = [al([128, 512], F32) for _ in range(2)]
        wuq = al([128, 2, 768], BF16)
        wuqs = al([128, 2, 768], BF16)
        wukv = al([128, 1024], BF16)
        cvb = [al([128, 1024], BF16) for _ in range(2)]
        a1 = Bump(p2_base)
        xb1 = [a1([128, D], F32) for _ in range(6)]
        hT1 = [a1([128, 8, 512], BF16) for _ in range(2)]
        wB = a1([128, 8, 384], BF16)
        wKR = a1([128, 8, 192], BF16)
        f1 = [a1([128, 512], F32) for _ in range(2)]
        sq1 = [a1([128, 512], F32) for _ in range(2)]
        rs1 = a1([128, 512], F32)
        sqh = [a1([128, 512], BF16) for _ in range(2)]
        sql = [a1([128, 512], BF16) for _ in range(2)]
        tb1 = [a1([128, 1024], F32) for _ in range(2)]
        tbi = a1([128, 1024], I32)
        posi = [a1([128, 512], I32) for _ in range(2)]
        kr1 = [a1([128, 512], F32) for _ in range(2)]

        dcount = [0]

        def PS(b):
            return psum[:, b, :]

        def load_cast(dst, src_ap, n_elem, key, eng_cast="gpsimd"):
            slot = dcount[0] % 2
            dcount[0] += 1
            st = wst[:, slot, 0:n_elem]
            shp = dst.shape
            if len(shp) == 3:
                st = st.rearrange("p (a b) -> p a b", b=shp[2])
            op("sync", "dma_start", out=st, in_=src_ap, writes=[("wst", slot)], dma_slot=f"wst{slot}")
            if eng_cast == "scalar":
                op("scalar", "activation", dst, st, AF.Copy, reads=[("wst", slot)], writes=[key])
            else:
                op(eng_cast, "tensor_copy", dst, st, reads=[("wst", slot)], writes=[key])

        op("sync", "dma_start", out=ident[:], in_=ident_d, writes=["ident"], dma_slot="c0")
        op("sync", "dma_start", out=bcol[:], in_=bcol_d, writes=["bcol"], dma_slot="c1")
        op("sync", "dma_start", out=ccol[:], in_=ccol_d, writes=["ccol"], dma_slot="c2")
        op("sync", "dma_start", out=lncol[:], in_=lncol_d, writes=["lncol"], dma_slot="c3")
        op("gpsimd", "memset", ones_b[:], 1.0, writes=["ones"])
        op("gpsimd", "memset", mhalf[:], -0.5, writes=["mhalf"])
        op("gpsimd", "memset", epsc[:], RMS_EPS, writes=["epsc"])
        op("gpsimd", "tensor_scalar", bcolh[:], bcol[:], 0.5, None, ALU.mult, reads=["bcol"], writes=["bcolh"])
        for k in range(8):
            load_cast(wB[:, k, :], w_in_d[k * 128:(k + 1) * 128, C_CQ:C_CQ + 384], 384, ("wB", k))
            load_cast(wKR[:, k, :], w_krp_d[k * 128:(k + 1) * 128, :], 192, ("wKR", k))

        ZC = ccol[:, 7:8]
        FLAG = ccol[:, 5:6]

        def ln_stats(xt, sidx, keyx):
            keys = ("st", sidx)
            b = sidx * 16
            st6 = stat[:, b:b + 12].rearrange("p (a b) -> p a b", b=6)
            op("vector", "bn_stats", st6[:, 0, :], xt[:, 0:512], reads=[keyx], writes=[keys])
            op("vector", "bn_stats", st6[:, 1, :], xt[:, 512:1024], reads=[keyx], writes=[keys])
            op("vector", "bn_aggr", stat[:, b + 12:b + 14], stat[:, b:b + 12], reads=[keys], writes=[keys])
            op("vector", "tensor_scalar", stat[:, b + 13:b + 14], stat[:, b + 13:b + 14], LN_EPS, None, ALU.add,
               reads=[keys], writes=[keys])
            op("gpsimd", "tensor_tensor", stat[:, b + 13:b + 14], stat[:, b + 13:b + 14], mhalf[:, 0:1], ALU.pow,
               reads=[keys, "mhalf"], writes=[keys])
            op("vector", "scalar_tensor_tensor", stat[:, b + 14:b + 15], stat[:, b + 12:b + 13], -1.0,
               stat[:, b + 13:b + 14], ALU.mult, ALU.mult, reads=[keys], writes=[keys])
            return stat[:, b + 13:b + 14], stat[:, b + 14:b + 15], keys

        def ln_stats4(x4, keysx):
            k4 = "st4"
            for b in range(4):
                st6 = stat4[:, b * 12:(b + 1) * 12].rearrange("p (a b) -> p a b", b=6)
                op("vector", "bn_stats", st6[:, 0, :], x4[:, b, 0:512], reads=[keysx[b]], writes=[k4])
                op("vector", "bn_stats", st6[:, 1, :], x4[:, b, 512:1024], reads=[keysx[b]], writes=[k4])
                op("vector", "bn_aggr", stat4[:, 48 + 2 * b:50 + 2 * b], stat4[:, b * 12:(b + 1) * 12], reads=[k4], writes=[k4])
            mv = stat4[:, 48:56].rearrange("p (b t) -> p b t", t=2)
            op("vector", "tensor_scalar", stat4[:, 56:60], mv[:, :, 1], LN_EPS, None, ALU.add, reads=[k4], writes=[k4])
            op("gpsimd", "tensor_tensor", stat4[:, 56:60], stat4[:, 56:60], mhalf[:, 0:4], ALU.pow, reads=[k4, "mhalf"], writes=[k4])
            op("vector", "scalar_tensor_tensor", stat4[:, 60:64], mv[:, :, 0], -1.0, stat4[:, 56:60], ALU.mult, ALU.mult,
               reads=[k4], writes=[k4])
            return [(stat4[:, 56 + b:57 + b], stat4[:, 60 + b:61 + b]) for b in range(4)], k4

        tr_ctr = [0]
        evac_all_act = True

        def ln_norm(xt, keyx, sidx):
            rstd, nmr, keys = ln_stats(xt, sidx, keyx)
            op("scalar", "activation", xt, xt, AF.Identity, bias=nmr, scale=rstd, reads=[keyx, keys], writes=[keyx])

        def tr_evac(xt, keyx, hT, hkey, b, banks=(0, 1)):
            for half in range(2):
                bank = banks[tr_ctr[0] % 2]
                tr_ctr[0] += 1
                for kk in range(4):
                    k = half * 4 + kk
                    op("tensor", "transpose", psum[:, bank, kk * 128:(kk + 1) * 128], xt[:, k * 128:(k + 1) * 128], ident[:],
                       reads=[keyx, "ident"], writes=[("ps", bank)])
                for kk in range(4):
                    k = half * 4 + kk
                    dst = hT[:, k, b * 128:(b + 1) * 128]
                    src = psum[:, bank, kk * 128:(kk + 1) * 128]
                    if half == 0 and not evac_all_act:
                        op("vector", "tensor_scalar", dst, src, lncol[:, k:k + 1], lncol[:, 8 + k:9 + k], ALU.mult, ALU.add,
                           reads=[("ps", bank), "lncol"], writes=[hkey])
                    else:
                        op("scalar", "activation", dst, src, AF.Identity, bias=lncol[:, 8 + k:9 + k], scale=lncol[:, k:k + 1],
                           reads=[("ps", bank), "lncol"], writes=[hkey])

        R = slice(64, 96)

        def p1_stages(g):
            own = (g % 2 == 1)
            so = g // 2
            hT = hT1[g % 2]
            hkey = ("hT1", g % 2)
            pi = posi[g % 2]
            t0, t1 = tb1[0], tb1[1]
            if own:
                sdst, cdst = sinT[R, so * 512:(so + 1) * 512], cosT[R, so * 512:(so + 1) * 512]
                skey, ckey = ("sinT", so), ("cosT", so)
            else:
                sdst, cdst = t1[R, 0:512], t1[R, 512:1024]
                skey = ckey = "tb1"

            def g1():
                op("vector", "tensor_copy", t0[R, 0:512], pi[R, :], reads=[("posi", g % 2)], writes=["tb0"])
                op("vector", "tensor_scalar", t0[R, 0:512], t0[R, 0:512], ccol[R, 3:4], None, ALU.mult,
                   reads=["tb0", "ccol"], writes=["tb0"])
                op("vector", "tensor_scalar", t0[R, 512:1024], t0[R, 0:512], 0.25, None, ALU.add, reads=["tb0"], writes=["tb0"])
                op("vector", "tensor_copy", tbi[R, :], t0[R, :], reads=["tb0"], writes=["tbi"])
                op("vector", "tensor_copy", t1[R, :], tbi[R, :], reads=["tbi"], writes=["tb1"])
                op("vector", "tensor_tensor", t0[R, :], t0[R, :], t1[R, :], ALU.subtract, reads=["tb0", "tb1"], writes=["tb0"])
                op("vector", "tensor_scalar", t1[R, :], t0[R, :], 0.5, None, ALU.is_gt, reads=["tb0"], writes=["tb1"])
                op("vector", "tensor_tensor", t0[R, :], t0[R, :], t1[R, :], ALU.subtract, reads=["tb0", "tb1"], writes=["tb0"])
                op("vector", "tensor_scalar", t1[R, :], t0[R, :], -0.5, None, ALU.is_lt, reads=["tb0"], writes=["tb1"])
                op("vector", "tensor_tensor", t0[R, :], t0[R, :], t1[R, :], ALU.add, reads=["tb0", "tb1"], writes=["tb0"])
                op("scalar", "activation", sdst, t0[R, 0:512], AF.Sin, scale=ccol[R, 4:5], reads=["tb0", "ccol"], writes=[skey])
                op("scalar", "activation", cdst, t0[R, 512:1024], AF.Sin, scale=ccol[R, 6:7], reads=["tb0", "ccol"], writes=[ckey])
                for j in range(2):
                    for k in range(8):
                        op("tensor", "matmul", psum[0:96, 2 + j, :], wKR[:, k, j * 96:(j + 1) * 96], hT[:, k, :],
                           start=(k == 0), stop=(k == 7), reads=[hkey, ("wKR", k)], writes=[("ps", 2 + j)])
                for k in range(8):
                    op("tensor", "matmul", PS(4), wB[:, k, 256:384], hT[:, k, :], start=(k == 0), stop=(k == 7),
                       reads=[hkey, ("wB", k)], writes=[("ps", 4)])
                if own:
                    for c in range(2):
                        for k in range(8):
                            op("tensor", "matmul", PS(5 + c), wB[:, k, c * 128:(c + 1) * 128], hT[:, k, :], start=(k == 0), stop=(k == 7),
                               reads=[hkey, ("wB", k)], writes=[("ps", 5 + c)])

            def sq_split(c, bank, bcols):
                op("vector", "tensor_scalar", f1[c][:], PS(bank), bcol[:, bcols:bcols + 1], None, ALU.add,
                   reads=[("ps", bank), "bcol"], writes=[("f1", c)])
                op("gpsimd", "tensor_tensor", sq1[c][:], f1[c][:], f1[c][:], ALU.mult, reads=[("f1", c)], writes=[("sq1", c)])
                op("gpsimd", "tensor_copy", sqh[c][:], sq1[c][:], reads=[("sq1", c)], writes=[("sqh", c)])
                op("gpsimd", "tensor_tensor", sql[c][:], sq1[c][:], sqh[c][:], ALU.subtract,
                   reads=[("sq1", c), ("sqh", c)], writes=[("sql", c)])

            def ones_red(n_chunk, norm_n):
                for c in range(n_chunk):
                    op("tensor", "matmul", PS(7), ones_b[:], sqh[c][:], start=(c == 0), stop=False,
                       reads=[("sqh", c), "ones"], writes=[("ps", 7)])
                    op("tensor", "matmul", PS(7), ones_b[:], sql[c][:], start=False, stop=(c == n_chunk - 1),
                       reads=[("sql", c), "ones"], writes=[("ps", 7)])
                op("scalar", "activation", rs1[:], PS(7), AF.Sqrt, bias=epsc[:, 0:1], scale=1.0 / norm_n,
                   reads=[("ps", 7), "epsc"], writes=["rs1"])

            def g2():
                ka, kb_ = kr1[0], kr1[1]
                op("vector", "scalar_tensor_tensor", ka[R, :], psum[R, 2, :], bcol[R, 39:40], cdst, ALU.add, ALU.mult,
                   reads=[("ps", 2), "bcol", ckey], writes=["kr1a"])
                op("vector", "scalar_tensor_tensor", kb_[R, :], psum[R, 3, :], bcol[R, 40:41], sdst, ALU.add, ALU.mult,
                   reads=[("ps", 3), "bcol", skey], writes=["kr1b"])
                op("vector", "tensor_tensor", KT[0][R, g * 512:(g + 1) * 512], ka[R, :], kb_[R, :], ALU.add,
                   reads=["kr1a", "kr1b"], writes=[("KTr0", g)])
                op("gpsimd", "tensor_copy", KT[1][R, g * 512:(g + 1) * 512], KT[0][R, g * 512:(g + 1) * 512],
                   reads=[("KTr0", g)], writes=[("KTr1", g)])
                sq_split(0, 4, 18)

            def g3():
                ones_red(1, 128.0)

            def g4():
                op("vector", "reciprocal", rs1[:], rs1[:], reads=["rs1"], writes=["rs1"])
                op("vector", "scalar_tensor_tensor", ckvT[:, g * 512:(g + 1) * 512], f1[0][:], ccol[:, 2:3], rs1[:],
                   ALU.mult, ALU.mult, reads=[("f1", 0), "rs1", "ccol"], writes=[("ckvT", g)])
                if own:
                    sq_split(0, 5, 16)
                    sq_split(1, 6, 17)

            def g5():
                if own:
                    ones_red(2, 256.0)

            def g6():
                if own:
                    op("vector", "reciprocal", rs1[:], rs1[:], reads=["rs1"], writes=["rs1"])
                    for c in range(2):
                        op("vector", "scalar_tensor_tensor", cqT[:, c, so * 512:(so + 1) * 512], f1[c][:], ccol[:, c:c + 1], rs1[:],
                           ALU.mult, ALU.mult, reads=[("f1", c), "rs1", "ccol"], writes=[("cqT", so, c)])

            return [g1, g2, g3, g4, g5, g6]

        if upto >= 1:
            jobs = [(g, b) for g in range(NG) for b in range(4)]
            due = {}
            SK1 = 4
            for i in range(len(jobs) + 10):
                if i < len(jobs):
                    g, b = jobs[i]
                    if b == 0:
                        op("sync", "dma_start", out=posi[g % 2][64:96, :], in_=pos_d[:, g * 512:(g + 1) * 512],
                           writes=[("posi", g % 2)], dma_slot=f"posi{g % 2}")
                    bi = i % 6
                    r0 = g * 512 + b * 128
                    op("sync", "dma_start", out=xb1[bi][:], in_=x_d[r0:r0 + 128, :], writes=[("xb1", bi)], dma_slot=f"xb1_{bi}")
                    ln_norm(xb1[bi][:], ("xb1", bi), bi)
                if SK1 <= i < len(jobs) + SK1:
                    g, b = jobs[i - SK1]
                    bi = (i - SK1) % 6
                    tr_evac(xb1[bi][:], ("xb1", bi), hT1[g % 2], ("hT1", g % 2), b)
                    if b == 3:
                        for dt_, fn in enumerate(p1_stages(g)):
                            due.setdefault(i + dt_, []).append(fn)
                for fn in due.pop(i, []):
                    fn()
            assert not due


        SC.barrier()
        if upto >= 2:
            for c in range(2):
                load_cast(wuq[:, c, :], w_uq_d[c * 128:(c + 1) * 128, :], 768, ("wuq", c))
                load_cast(wuqs[:, c, :], w_uqs_d[c * 128:(c + 1) * 128, :], 768, ("wuqs", c))
            load_cast(wukv[:], w_ukv_d, 1024, "wukv")
            for vb in range(2):
                op("gpsimd", "memset", Vb[vb][:, :, 64:128], 1.0, writes=[("Vones", vb)])

            SCALE_B = 96.0 ** -0.5
            cv_i = [0]

            def conv_step():
                i = cv_i[0]
                if i >= NCH:
                    return
                cv_i[0] += 1
                kind, c0, nk = CH[i]
                slot = i % 2
                n = nk * 128
                src = WSRC[kind][:, c0:c0 + 128].rearrange("(k p) c -> p k c", p=128)
                op("sync", "dma_start", out=wst[:, slot, 0:n].rearrange("p (k c) -> p k c", c=128), in_=src,
                   writes=[("wst", slot)], dma_slot=f"wst{slot}")
                op("gpsimd", "tensor_copy", cvb[slot][:, 0:n], wst[:, slot, 0:n], reads=[("wst", slot)], writes=[("cvb", slot)])
                op("sync", "dma_start", out=wscr_d[i, :, 0:n], in_=cvb[slot][:, 0:n], reads=[("cvb", slot)],
                   writes=[("wscr", i)], dma_slot=f"cvo{slot}")
            gctr = [0]

            def gen_steps(h):
                buf = h % 2
                steps = []
                for g in range(NG):
                    def kstep(g=g):
                        pb = 6 + (gctr[0] % 2)
                        gctr[0] += 1
                        op("tensor", "matmul", psum[0:64, pb, :], wukv[:, h * 128:h * 128 + 64], ckvT[:, g * 512:(g + 1) * 512],
                           start=True, stop=True, reads=[("ckvT", g), "wukv"], writes=[("ps", pb)])
                        op("vector", "tensor_copy", KT[buf][0:64, g * 512:(g + 1) * 512], psum[0:64, pb, :],
                           reads=[("ps", pb)], writes=[("KT", buf, g)])
                    steps.append(kstep)
                for t8 in range(NT // 8):
                    def vstep(t8=t8):
                        pb = 6 + (gctr[0] % 2)
                        gctr[0] += 1
                        for tt in range(8):
                            t = t8 * 8 + tt
                            op("tensor", "matmul", psum[:, pb, tt * 64:(tt + 1) * 64], ckvT[:, t * 128:(t + 1) * 128],
                               wukv[:, h * 128 + 64:h * 128 + 128], start=True, stop=True,
                               reads=[("ckvT", t // 4), "wukv"], writes=[("ps", pb)])
                        op("vector", "tensor_copy", Vb[buf][:, t8 * 8:(t8 + 1) * 8, 0:64],
                           psum[:, pb, :].rearrange("p (a b) -> p a b", b=64), reads=[("ps", pb)], writes=[("V", buf, t8)])
                    steps.append(vstep)
                return steps

            def gen_q(h, s, qb):
                pq1 = 6 + (gctr[0] % 2)
                pq2 = 6 + ((gctr[0] + 1) % 2)
                gctr[0] += 2
                for c in range(2):
                    op("tensor", "matmul", psum[0:96, pq1, :], wuq[:, c, h * 96:(h + 1) * 96], cqT[:, c, s * 512:(s + 1) * 512],
                       start=(c == 0), stop=(c == 1), reads=[("cqT", s, c), ("wuq", c)], writes=[("ps", pq1)])
                for c in range(2):
                    op("tensor", "matmul", psum[0:96, pq2, :], wuqs[:, c, h * 96:(h + 1) * 96], cqT[:, c, s * 512:(s + 1) * 512],
                       start=(c == 0), stop=(c == 1), reads=[("cqT", s, c), ("wuqs", c)], writes=[("ps", pq2)])
                q = QT[qb]
                op("vector", "tensor_copy", q[0:64, :], psum[0:64, pq1, :], reads=[("ps", pq1)], writes=[("QT", qb)])
                ta, tb_ = qtmp[0], qtmp[1]
                op("vector", "tensor_tensor", ta[R, :], psum[R, pq1, :], cosT[R, s * 512:(s + 1) * 512], ALU.mult,
                   reads=[("ps", pq1), ("cosT", s)], writes=["qta"])
                op("vector", "tensor_tensor", tb_[R, :], psum[R, pq2, :], sinT[R, s * 512:(s + 1) * 512], ALU.mult,
                   reads=[("ps", pq2), ("sinT", s)], writes=["qtb"])
                op("vector", "tensor_tensor", q[R, :], ta[R, :], tb_[R, :], ALU.add, reads=["qta", "qtb"], writes=[("QT", qb)])

            for st_ in gen_steps(0):
                st_()
            qctr = 0
            uctr = 0
            for h in range(8):
                buf = h % 2
                nxt = gen_steps(h + 1) if h < 7 else []
                gen_q(h, 0, qctr % 2)
                for s in range(NP):
                    qb = qctr % 2
                    qctr += 1
                    q = QT[qb]
                    nfull = 8 * s + 4
                    units = []
                    for i in range(nfull // 2):
                        units.append([(2 * i, 0, 512, 0), (2 * i + 1, 0, 512, 512)])
                    t0_ = nfull
                    units.append([(t0_, 0, 512, 0), (t0_ + 1, 128, 384, 512)])
                    units.append([(t0_ + 2, 256, 256, 0), (t0_ + 3, 384, 128, 256)])
                    po = 4 + (s % 2)
                    pend = None
                    first_pv = True

                    for ui, u in enumerate(units):
                        sb_ = uctr % 2
                        pt_i = uctr % 3
                        uctr += 1
                        width = u[-1][3] + u[-1][2]
                        for (t, q0, n, pc) in u:
                            bank = 2 * sb_ + (pc // 512)
                            op("tensor", "matmul", psum[:, bank, (pc % 512):(pc % 512) + n], KT[buf][0:96, t * 128:(t + 1) * 128],
                               q[0:96, q0:q0 + n], start=True, stop=True,
                               reads=[("KT", buf, t // 4), ("KTr%d" % buf, t // 4), ("QT", qb)], writes=[("ps", bank)])
                        sview = psum[:, 2 * sb_:2 * sb_ + 2, :].rearrange("p a b -> p (a b)")[:, 0:width]
                        if u[0][0] < 4:
                            op("scalar", "activation", PT[pt_i][:, 0:width], sview, AF.Exp, bias=FLAG, scale=SCALE_B,
                               reads=[("ps", 2 * sb_), ("ps", 2 * sb_ + 1), "ccol"], writes=[("PT", pt_i)])
                        else:
                            op("scalar", "activation", PT[pt_i][:, 0:width], sview, AF.Exp, scale=SCALE_B,
                               reads=[("ps", 2 * sb_), ("ps", 2 * sb_ + 1)], writes=[("PT", pt_i)])
                        if u[0][0] >= nfull:
                            for (t, q0, n, pc) in u:
                                op("gpsimd", "memset", PT[pt_i][64:128, pc:pc + 64], 0.0, writes=[("PT", pt_i)])
                        if pend is not None:
                            for (t, q0, n, pc) in pend[0]:
                                op("tensor", "matmul", psum[:, po, q0:q0 + n], Vb[buf][:, t, :], PT[pend[1]][:, pc:pc + n],
                                   start=first_pv, stop=False, skip_group_check=True,
                                   reads=[("V", buf, t // 8), ("Vones", buf), ("PT", pend[1])], writes=[("ps", po)])
                                first_pv = False
                        pend = (u, pt_i)
                        if nxt:
                            nxt.pop(0)()
                        if uctr % 4 == 0:
                            conv_step()
                        if ui == 0 and s + 1 < NP:
                            gen_q(h, s + 1, qctr % 2)
                    for (t, q0, n, pc) in pend[0]:
                        op("tensor", "matmul", psum[:, po, q0:q0 + n], Vb[buf][:, t, :], PT[pend[1]][:, pc:pc + n],
                           start=first_pv, stop=False, skip_group_check=True,
                           reads=[("V", buf, t // 8), ("Vones", buf), ("PT", pend[1])], writes=[("ps", po)])
                        first_pv = False
                    rc = rcb[s % 2]
                    c = h // 2
                    lo = (h % 2) * 64
                    op("vector", "reciprocal", rc[0:64, :], psum[64:128, po, :], reads=[("ps", po)], writes=[("rc", s % 2)])
                    op("vector", "tensor_tensor", ybT[lo:lo + 64, c, s * 512:(s + 1) * 512], psum[0:64, po, :], rc[0:64, :], ALU.mult,
                       reads=[("ps", po), ("rc", s % 2)], writes=[("ybT", c, s)])
                while nxt:
                    nxt.pop(0)()
            while cv_i[0] < NCH:
                conv_step()

        stores = []
        if dbg and upto >= 2:
            stores.append(op("sync", "dma_start", out=dbg_yb_d, in_=ybT[:],
                             reads=[("ybT", c, s) for c in range(4) for s in range(NP)], dma_slot="dbg0"))
        SC.barrier()

        a3 = Bump(0)
        BT = a3([128, 8, 640], F32)
        lnbc = a3([128, 4, D], F32)
        xo = a3([128, 4, D], F32)
        xh = [a3([128, D], F32) for _ in range(3)]
        hTe = a3([128, 8, 512], BF16)
        hTw = a3([128, 8, 512], BF16)
        wch = [a3([128, 8, 128], BF16) for _ in range(7)]
        wbufs = list(wch) + [wst[:, sl, hf * 512:(hf + 1) * 512].bitcast(BF16).rearrange("p (k c) -> p k c", c=128)
                             for sl in range(2) for hf in range(2)]
        aqT = [a3([128, 512], BF16) for _ in range(2)]
        akT = [a3([128, 1024], BF16) for _ in range(2)]
        Va = [a3([128, 8, 2, 128], BF16) for _ in range(2)]
        silu2 = [a3([128, 512], F32) for _ in range(2)]
        silu2b = [a3([128, 512], F32) for _ in range(2)]
        zbt = [a3([128, 512], F32) for _ in range(2)]
        tzt = [a3([128, 512], F32) for _ in range(2)]
        Stmp = [a3([128, 512], F32) for _ in range(3)]
        PTa = [a3([128, 512], BF16) for _ in range(4)]
        rca = [a3([128, 512], F32) for _ in range(2)]
        att = [a3([128, 512], F32) for _ in range(2)]
        zaT = a3([128, 4, 512], BF16)
        zbT = a3([128, 4, 512], BF16)
        tat = [a3([128, 512], F32) for _ in range(2)]
        tbt = [a3([128, 512], F32) for _ in range(2)]
        t1t = [a3([128, 512], F32) for _ in range(2)]
        t2t = [a3([128, 512], F32) for _ in range(2)]
        mixT = a3([128, 8, 512], BF16)
        amk = a3([128, 640], F32)

        if upto >= 3:
            op("sync", "dma_start", out=BT[:], in_=bt_d, writes=["BT"], dma_slot="c4")
            op("sync", "dma_start", out=amk[:], in_=amask_d, writes=["amk"], dma_slot="c5")
            op("sync", "dma_start", out=lnbc[:], in_=lnbc_d, writes=["lnbc"], dma_slot="c6")
            for h in range(8):
                op("gpsimd", "tensor_tensor", BT[:, h, :], BT[:, h, :], amk[:], ALU.add, reads=["BT", "amk"], writes=["BT"])
            op("gpsimd", "tensor_scalar", lnbc[:, 0:2, :], lnbc[:, 0:2, :], ALPHA, None, ALU.mult, reads=["lnbc"], writes=["lnbc"])
            for vb in range(2):
                op("gpsimd", "memset", Va[vb][:, :, :, 64:128], 1.0, writes=[("VaOnes", vb)])

            wctr = [0]
            cast_ctr = [0]

            wissued = [0]
            TOTCH = NCH * NP

            def stream_chunk(src_d, c0, nk):
                i = wctr[0]
                wctr[0] += 1
                ci = i % NCH
                assert CH[ci][1] == c0 and CH[ci][2] == nk, (ci, CH[ci], c0, nk)
                while wissued[0] < min(TOTCH, i + len(wbufs)):
                    ii = wissued[0]
                    wissued[0] += 1
                    cj = ii % NCH
                    nkk = CH[cj][2]
                    jb = ii % len(wbufs)
                    op("sync", "dma_start", out=wbufs[jb][:, 0:nkk, :],
                       in_=wscr_d[cj, :, 0:nkk * 128].rearrange("p (k c) -> p k c", c=128),
                       reads=[("wscr", cj)], writes=[("wch", jb)], dma_slot=f"wch{jb}")
                j = i % len(wbufs)
                return wbufs[j], ("wch", j)

            ipb = [0]

            def inproj(wt, wkey, hT, hkey):
                bank = ipb[0] % 2
                ipb[0] += 1
                for k in range(8):
                    op("tensor", "matmul", PS(bank), wt[:, k, :], hT[:, k, :], start=(k == 0), stop=(k == 7),
                       reads=[wkey, hkey], writes=[("ps", bank)])
                return bank

            SCALE_A = 0.125
            JORDER = [3, 4, 2, 5, 1, 6, 0, 7]
            JN = {0: (0, 128), 1: (0, 256), 2: (0, 384), 3: (0, 512), 4: (0, 512), 5: (128, 384), 6: (256, 256), 7: (384, 128)}
            sctr = [0]
            evc = [0]

            def prep_jobs(s):
                e_g, w_g = 2 * s, 2 * s + 1
                return [(g_, hT_, hk_, b) for (g_, hT_, hk_) in ((e_g, hTe, "hTe"), (w_g, hTw, "hTw")) for b in range(4)]

            def prepA(job, idx):
                g_, hT_, hk_, b = job
                xi = idx % 3
                r0 = g_ * 512 + b * 128
                op("sync", "dma_start", out=xh[xi][:], in_=x_d[r0:r0 + 128, :], writes=[("xh", xi)], dma_slot=f"xh{xi}")
                ln_norm(xh[xi][:], ("xh", xi), xi)

            def prepB(job, idx):
                g_, hT_, hk_, b = job
                xi = idx % 3
                tr_evac(xh[xi][:], ("xh", xi), hT_, hk_, b, banks=(2, 3))

            def post_ln(s):
                cols, k4 = ln_stats4(xo, [("xo", b) for b in range(4)])
                for b in range(4):
                    xt = xo[:, b, :]
                    rstd, nmr = cols[b]
                    op("scalar", "activation", xt, xt, AF.Identity, bias=nmr, scale=rstd, reads=[("xo", b), k4], writes=[("xo", b)])
                    op("gpsimd", "tensor_tensor", xt, xt, lnbc[:, 2, :], ALU.mult, reads=[("xo", b), "lnbc"], writes=[("xo", b)])
                    op("gpsimd", "tensor_tensor", xt, xt, lnbc[:, 3, :], ALU.add, reads=[("xo", b), "lnbc"], writes=[("xo", b)])
                    r0 = s * 512 + b * 128
                    stores.append(op("sync", "dma_start", out=out_d[r0:r0 + 128, :], in_=xt, reads=[("xo", b)], dma_slot=f"st{b}"))

            def resid_load(s):
                w_g = 2 * s + 1
                op("sync", "dma_start", out=xo[:], in_=x_d[w_g * 512:(w_g + 1) * 512, :].rearrange("(b p) d -> p b d", p=128),
                   writes=[("xo", b) for b in range(4)], dma_slot="xo")

            def resid_prep(s):
                cols, k4 = ln_stats4(xo, [("xo", b) for b in range(4)])
                for b in range(4):
                    xt = xo[:, b, :]
                    rstd, nmr = cols[b]
                    op("scalar", "activation", xt, xt, AF.Identity, bias=nmr, scale=rstd, reads=[("xo", b), k4], writes=[("xo", b)])
                    op("gpsimd", "tensor_tensor", xt, xt, lnbc[:, 0, :], ALU.mult, reads=[("xo", b), "lnbc"], writes=[("xo", b)])
                    op("gpsimd", "tensor_tensor", xt, xt, lnbc[:, 1, :], ALU.add, reads=[("xo", b), "lnbc"], writes=[("xo", b)])

            xhc = [0]
            fin_defer = []
            hctr = [0]
            if p3cut >= 1:
                j0 = prep_jobs(0)
                for i in range(10):
                    if i < 8:
                        prepA(j0[i], i)
                    if i >= 2:
                        prepB(j0[i - 2], i - 2)
            for s in range(NP):
                if p3cut < 2:
                    continue
                def inproj_jobs(c):
                    cb = c % 2
                    hold = {}

                    def j_ak(hi):
                        def f():
                            if hi == 0:
                                hold["ak"] = stream_chunk(w_in_d, C_AK + c * 128, 8)
                            wt, wkey = hold["ak"]
                            hT_, hk_ = ((hTe, "hTe"), (hTw, "hTw"))[hi]
                            bank = inproj(wt, wkey, hT_, hk_)
                            op("scalar", "activation", akT[cb][:, hi * 512:(hi + 1) * 512], PS(bank), AF.Identity, bias=bcol[:, 4 + c:5 + c],
                               reads=[("ps", bank), "bcol"], writes=[("akT", cb)])
                        return f

                    def j_aq():
                        wt, wkey = stream_chunk(w_in_d, C_AQ + c * 128, 8)
                        bank = inproj(wt, wkey, hTw, "hTw")
                        op("scalar", "activation", aqT[cb][:], PS(bank), AF.Identity, bias=bcol[:, 0 + c:1 + c],
                           reads=[("ps", bank), "bcol"], writes=[("aqT", cb)])

                    def j_av(half, b):
                        def f():
                            if half == 0 and b == 0:
                                hold["av"] = stream_chunk(w_in_d, C_AV + c * 128, 8)
                            wt, wkey = hold["av"]
                            hT_, hk_ = ((hTe, "hTe"), (hTw, "hTw"))[half]
                            bank = 2 + half
                            for k in range(8):
                                op("tensor", "matmul", psum[:, bank, b * 128:(b + 1) * 128], hT_[:, k, b * 128:(b + 1) * 128], wt[:, k, :],
                                   start=(k == 0), stop=(k == 7), reads=[wkey, hk_], writes=[("ps", bank)])
                            if b == 3:
                                op("scalar", "activation", Va[cb][:, half * 4:(half + 1) * 4, :, 0:64],
                                   psum[:, bank, :].rearrange("p (b h d) -> p b h d", h=2, d=64), AF.Copy,
                                   reads=[("ps", bank)], writes=[("Va", cb)])
                        return f

                    def j_az():
                        wt, wkey = stream_chunk(w_in_d, C_AZ + c * 128, 8)
                        bank = inproj(wt, wkey, hTw, "hTw")
                        op("scalar", "activation", zbt[cb][:], PS(bank), AF.Identity, bias=bcol[:, 12 + c:13 + c],
                           reads=[("ps", bank), "bcol"], writes=[("zbt", cb)])
                        op("scalar", "activation", tzt[cb][:], zbt[cb][:], AF.Tanh, scale=0.5,
                           reads=[("zbt", cb)], writes=[("tzt", cb)])
                        op("vector", "scalar_tensor_tensor", silu2[cb][:], tzt[cb][:], 1.0, zbt[cb][:], ALU.add, ALU.mult,
                           reads=[("tzt", cb), ("zbt", cb)], writes=[("silu2", cb)])
                    return ([j_ak(0), j_ak(1), j_aq] + [j_av(hf, b) for hf in range(2) for b in range(4)] + [j_az])

                def bgate_jobs():
                    jobs = []
                    for c in range(4):
                        def j_bz(c=c):
                            cb = c % 2
                            wt, wkey = stream_chunk(w_in_d, C_BZ + c * 128, 8)
                            bank = inproj(wt, wkey, hTw, "hTw")
                            op("scalar", "activation", zbt[cb][:], PS(bank), AF.Identity, bias=bcol[:, 19 + c:20 + c],
                               reads=[("ps", bank), "bcol"], writes=[("zbt", cb)])
                            op("scalar", "activation", tzt[cb][:], zbt[cb][:], AF.Tanh, scale=0.5,
                               reads=[("zbt", cb)], writes=[("tzt", cb)])
                            op("vector", "scalar_tensor_tensor", silu2b[cb][:], tzt[cb][:], 1.0, zbt[cb][:], ALU.add, ALU.mult,
                               reads=[("tzt", cb), ("zbt", cb)], writes=[("silu2b", cb)])
                            op("gpsimd", "tensor_tensor", zbT[:, c, :], ybT[:, c, s * 512:(s + 1) * 512], silu2b[cb][:], ALU.mult,
                               reads=[("ybT", c, s), ("silu2b", cb)], writes=[("zbT", c)])
                        jobs.append(j_bz)
                    return jobs

                def attention(c, side_jobs):
                    cb = c % 2
                    steps = []
                    for hh in range(2):
                        po = 6 + (hctr[0] % 2)
                        hctr[0] += 1
                        for ji, j in enumerate(JORDER):
                            steps.append((hh, po, j, ji == 0, ji == 7))

                    def qk_stage(st):
                        hh, po, j, first, last = st
                        h = 2 * c + hh
                        lo = hh * 64
                        q0, n = JN[j]
                        u0 = q0 - (j - 4) * 128
                        sbk = 4 + (sctr[0] % 2)
                        si = sctr[0] % 3
                        pi_ = sctr[0] % 4
                        sctr[0] += 1
                        op("tensor", "matmul", psum[:, sbk, 0:n], akT[cb][lo:lo + 64, j * 128:(j + 1) * 128], aqT[cb][lo:lo + 64, q0:q0 + n],
                           start=True, stop=True, reads=[("akT", cb), ("aqT", cb)], writes=[("ps", sbk)])
                        op("vector", "scalar_tensor_tensor", Stmp[si][:, 0:n], psum[:, sbk, 0:n], SCALE_A, BT[:, h, u0:u0 + n], ALU.mult, ALU.add,
                           reads=[("ps", sbk), "BT"], writes=[("Stmp", si)])
                        if s == 0 and j < 4:
                            op("scalar", "activation", PTa[pi_][:, 0:n], Stmp[si][:, 0:n], AF.Exp, bias=FLAG,
                               reads=[("Stmp", si), "ccol"], writes=[("PTa", pi_)])
                        else:
                            op("scalar", "activation", PTa[pi_][:, 0:n], Stmp[si][:, 0:n], AF.Exp,
                               reads=[("Stmp", si)], writes=[("PTa", pi_)])
                        if fin_defer:
                            fin_defer.pop(0)()
                        return pi_

                    def pv_stage(st, pi_):
                        hh, po, j, first, last = st
                        h = 2 * c + hh
                        lo = hh * 64
                        q0, n = JN[j]
                        op("tensor", "matmul", psum[:, po, q0:q0 + n], Va[cb][:, j, hh, :], PTa[pi_][:, 0:n],
                           start=first, stop=False, skip_group_check=True,
                           reads=[("Va", cb), ("VaOnes", cb), ("PTa", pi_)], writes=[("ps", po)])
                        if last:
                            ri = h % 2
                            for pc_ in range(4):
                                def fin(pc_=pc_, ri=ri, lo=lo, po=po, c=c, cb=cb):
                                    cs_ = slice(pc_ * 128, (pc_ + 1) * 128)
                                    op("vector", "reciprocal", rca[ri][0:64, cs_], psum[64:128, po, cs_], reads=[("ps", po)], writes=[("rca", ri)])
                                    op("vector", "tensor_tensor", att[ri][lo:lo + 64, cs_], psum[0:64, po, cs_], rca[ri][0:64, cs_], ALU.mult,
                                       reads=[("ps", po), ("rca", ri)], writes=[("att", ri)])
                                    op("vector", "scalar_tensor_tensor", zaT[lo:lo + 64, c, cs_], att[ri][lo:lo + 64, cs_], bcol[lo:lo + 64, 8 + c:9 + c],
                                       silu2[cb][lo:lo + 64, cs_], ALU.add, ALU.mult,
                                       reads=[("att", ri), "bcol", ("silu2", cb)], writes=[("zaT", c)])
                                fin_defer.append(fin)

                    LOOK = 2
                    pis = {}
                    side = list(side_jobs)
                    every = 1 if len(side) > 8 else max(1, (len(steps) - 2) // max(1, len(side)))
                    for i in range(len(steps) + LOOK):
                        if i < len(steps):
                            pis[i] = qk_stage(steps[i])
                        if i - LOOK >= 0:
                            pv_stage(steps[i - LOOK], pis[i - LOOK])
                        if side and i >= 1 and (i - 1) % every == 0:
                            side.pop(0)()
                    while side:
                        side.pop(0)()

                for jb in inproj_jobs(0):
                    jb()
                if s > 0 and p3cut >= 6:
                    post_ln(s - 1)
                resid_load(s)
                for c in range(4):
                    attention(c, inproj_jobs(c + 1) if c < 3 else (bgate_jobs() if p3cut >= 3 else []))
                while fin_defer:
                    fin_defer.pop(0)()
                if p3cut < 3:
                    continue
                resid_prep(s)

                if p3cut < 4:
                    continue
                for oc in range(8):
                    ob = oc % 2
                    wt, wkey = stream_chunk(w_in_d, C_GA + oc * 128, 8)
                    bank = inproj(wt, wkey, hTw, "hTw")
                    op("scalar", "activation", tat[ob][:], PS(bank), AF.Tanh, bias=bcolh[:, 23 + oc:24 + oc], scale=0.5,
                       reads=[("ps", bank), "bcolh"], writes=[("tat", ob)])
                    wt, wkey = stream_chunk(w_pa_d, oc * 128, 4)
                    for c in range(4):
                        op("tensor", "matmul", PS(7), wt[:, c, :], zaT[:, c, :], start=(c == 0), stop=(c == 3),
                           reads=[wkey, ("zaT", c)], writes=[("ps", 7)])
                    op("vector", "scalar_tensor_tensor", t1t[ob][:], tat[ob][:], 1.0, PS(7), ALU.add, ALU.mult,
                       reads=[("tat", ob), ("ps", 7)], writes=[("t1t", ob)])
                    wt, wkey = stream_chunk(w_in_d, C_GB + oc * 128, 8)
                    bank = inproj(wt, wkey, hTw, "hTw")
                    op("scalar", "activation", tbt[ob][:], PS(bank), AF.Tanh, bias=bcolh[:, 31 + oc:32 + oc], scale=0.5,
                       reads=[("ps", bank), "bcolh"], writes=[("tbt", ob)])
                    wt, wkey = stream_chunk(w_pb_d, oc * 128, 4)
                    for c in range(4):
                        op("tensor", "matmul", PS(6), wt[:, c, :], zbT[:, c, :], start=(c == 0), stop=(c == 3),
                           reads=[wkey, ("zbT", c)], writes=[("ps", 6)])
                    op("vector", "scalar_tensor_tensor", t2t[ob][:], tbt[ob][:], 1.0, PS(6), ALU.add, ALU.mult,
                       reads=[("tbt", ob), ("ps", 6)], writes=[("t2t", ob)])
                    op("gpsimd", "tensor_tensor", mixT[:, oc, :], t1t[ob][:], t2t[ob][:], ALU.add,
                       reads=[("t1t", ob), ("t2t", ob)], writes=[("mixT", oc)])
                if p3cut < 5:
                    continue
                pj = prep_jobs(s + 1) if s + 1 < NP else []
                if pj:
                    prepA(pj[0], 0)
                    prepA(pj[1], 1)
                for cc in range(8):
                    if pj and cc + 2 < 8:
                        prepA(pj[cc + 2], cc + 2)
                    if pj:
                        prepB(pj[cc], cc)
                    wt, wkey = stream_chunk(w_out_d, cc * 128, 8)
                    bank = 4 + (cc % 2)
                    for b in range(4):
                        for oc in range(8):
                            op("tensor", "matmul", psum[:, bank, b * 128:(b + 1) * 128], mixT[:, oc, b * 128:(b + 1) * 128], wt[:, oc, :],
                               start=(oc == 0), stop=(oc == 7), reads=[wkey, ("mixT", oc)], writes=[("ps", bank)])
                    op("vector", "scalar_tensor_tensor", xo[:, :, cc * 128:(cc + 1) * 128],
                       psum[:, bank, :].rearrange("p (b d) -> p b d", d=128), 0.25, xo[:, :, cc * 128:(cc + 1) * 128], ALU.mult, ALU.add,
                       reads=[("ps", bank)] + [("xo", b) for b in range(4)], writes=[("xo", b) for b in range(4)])
                if p3cut < 6:
                    continue
                if s == NP - 1:
                    post_ln(s)


        SC.emit(nc, es, final_waits=stores)
    return nc


def host_consts():
    c = {}
    c["ident"] = np.eye(128, dtype=np.float32)
    k = np.arange(128)[:, None]
    u = np.arange(640)[None, :]
    kc = k // 64
    uc = u // 64
    vis = (kc <= uc) & (kc >= uc - 8)
    c["amask"] = np.where(vis, 0.0, NEG).astype(np.float32)
    c["bt_idx"] = (np.clip(u - k, -128, 128) + 128).astype(np.int64)
    return c


def core_inputs(inp, b, p, S=SEQ, consts=None):
    cs = consts or host_consts()
    f32 = np.float32
    x = np.asarray(inp["x"][b], dtype=f32)
    pos = np.asarray(inp["positions"][b], dtype=np.int32)
    if p == 0:
        xl = np.concatenate([np.zeros((512, D), f32), x[:S - 512]], axis=0)
        pl = np.concatenate([np.zeros((512,), np.int32), pos[:S - 512]], axis=0)
    else:
        xl, pl = x[:S], pos[:S]
    b_in = np.asarray(inp["b_in"][0], f32)
    w_in = np.ascontiguousarray(np.asarray(inp["w_in"][0], f32))
    starts = ([C_AQ + 128 * i for i in range(4)] + [C_AK + 128 * i for i in range(4)] + [C_AV + 128 * i for i in range(4)]
              + [C_AZ + 128 * i for i in range(4)] + [C_CQ, C_CQ + 128, C_CKV] + [C_BZ + 128 * i for i in range(4)]
              + [C_GA + 128 * i for i in range(8)] + [C_GB + 128 * i for i in range(8)])
    bcol = np.zeros((128, 44), f32)
    for i, st in enumerate(starts):
        bcol[:, i] = b_in[st:st + 128]
    bkr = b_in[C_KR:C_KR + 32]
    bcol[64:96, 39] = bkr
    bcol[64:80, 40] = bkr[16:32]
    bcol[80:96, 40] = bkr[0:16]
    ccol = np.zeros((128, 8), f32)
    qg = np.asarray(inp["q_norm_g"][0], f32)
    ccol[:, 0] = qg[0:128]
    ccol[:, 1] = qg[128:256]
    ccol[:, 2] = np.asarray(inp["kv_norm_g"][0], f32)
    inv_freq = (10000.0 ** (-np.arange(16, dtype=np.float32) / 16)).astype(f32)
    r = np.arange(32)
    ccol[64:96, 3] = (inv_freq[r % 16].astype(np.float64) / TWO_PI).astype(f32)
    tp = np.float32(6.28318)
    ccol[64:80, 4] = -tp
    ccol[80:96, 4] = tp
    ccol[64:96, 6] = tp
    ccol[:, 5] = NEG if p == 0 else 0.0
    lncol = np.zeros((128, 16), f32)
    lncol[:, 0:8] = np.asarray(inp["ln_in_g"], f32).reshape(8, 128).T
    lncol[:, 8:16] = np.asarray(inp["ln_in_b"], f32).reshape(8, 128).T
    lnbc = np.stack([np.broadcast_to(np.asarray(v, f32).reshape(1, D), (128, D)) for v in
                     (inp["ln_in_g"], inp["ln_in_b"], inp["ln_post_g"][0], inp["ln_post_b"][0])], axis=1)
    rb = np.asarray(inp["rel_bias"][0], f32)
    bt = np.ascontiguousarray(np.transpose(rb[cs["bt_idx"]], (0, 2, 1)))
    w_krp = np.zeros((D, 192), f32)
    w_krp[:, 64:96] = w_in[:, C_KR:C_KR + 32]
    w_krp[:, 160:176] = w_in[:, C_KR + 16:C_KR + 32]
    w_krp[:, 176:192] = w_in[:, C_KR:C_KR + 16]
    w_uq = np.ascontiguousarray(np.asarray(inp["w_uq"][0], f32))
    w_uqs = np.zeros_like(w_uq)
    for h in range(8):
        w_uqs[:, h * 96 + 64:h * 96 + 80] = w_uq[:, h * 96 + 80:h * 96 + 96]
        w_uqs[:, h * 96 + 80:h * 96 + 96] = w_uq[:, h * 96 + 64:h * 96 + 80]
    return {
        "x": np.ascontiguousarray(xl), "pos32": np.ascontiguousarray(np.broadcast_to(pl[None, :], (32, S))),
        "ident": cs["ident"], "bcol": bcol, "ccol": ccol, "lncol": lncol, "lnbc": np.ascontiguousarray(lnbc),
        "bt": bt, "amask": cs["amask"], "w_in": w_in, "w_krp": w_krp, "w_uq": w_uq, "w_uq_sw": w_uqs,
        "w_ukv": np.ascontiguousarray(np.asarray(inp["w_ukv"][0], f32)),
        "w_proj_a": np.ascontiguousarray(np.asarray(inp["w_proj_a"][0], f32)),
        "w_proj_b": np.ascontiguousarray(np.asarray(inp["w_proj_b"][0], f32)),
        "w_out": np.ascontiguousarray(np.asarray(inp["w_out"][0], f32)),
    }


def assemble(results, n_batch, S=SEQ):
    out = np.zeros((n_batch, S, D), np.float32)
    for b in range(n_batch):
        for p in range(2):
            o = np.asarray(results[b * 2 + p]["out"], np.float32)
            for s in range(S // 1024):
                gg = 2 * s + p
                out[b, gg * 512:(gg + 1) * 512, :] = o[s * 512:(s + 1) * 512, :]
    return out


_NC_CACHE = {}


def kernel(**inputs):
    inp = {k: np.asarray(v) for k, v in inputs.items()}
    nb = inp["x"].shape[0]
    S = inp["x"].shape[1]
    if S not in _NC_CACHE:
        _NC_CACHE[S] = build_nc(S)
    nc = _NC_CACHE[S]
    cs = host_consts()
    in_maps = [core_inputs(inp, b, p, S, cs) for b in range(nb) for p in range(2)]
    res = run_bass_kernel_spmd(nc, in_maps, core_ids=list(range(2 * nb)))
    return assemble(res.results, nb, S)
```
